# Optimizing a Trainium2 kernel written in Bass

```python
import math
import jax, jax.numpy as jnp
from jax import lax
import numpy as np

D_MODEL = 1024
BATCH = 8
SEQ = 2048
DEPTH = 2

GRID_W = 64
NA_HEADS = 8
NA_HEAD_DIM = 64
NA_WIN_ROWS = 8
NA_WIN_COLS = 16
NA_WIDTH = NA_HEADS * NA_HEAD_DIM
DIFF_HEADS = 4
DIFF_QK_DIM = 64
DIFF_V_DIM = 2 * DIFF_QK_DIM
DIFF_QK_WIDTH = DIFF_HEADS * 2 * DIFF_QK_DIM
DIFF_V_WIDTH = DIFF_HEADS * DIFF_V_DIM
Q_BLOCK = 128
T5_BUCKETS = 32
T5_MAX_DIST = 128
D_FF = 4 * D_MODEL
N_GATES = 2
W_IN_COLS = 3 * NA_WIDTH + 2 * DIFF_QK_WIDTH + DIFF_V_WIDTH + N_GATES * D_MODEL
RMS_EPS = 1e-6
NEG_INF = -1e30

kernel_name = "hybrid_na_diffattn_gated_encoder"


def rmsnorm(x, g):
    xf = x.astype(jnp.float32)
    y = xf * lax.rsqrt(jnp.mean(xf * xf, axis=-1, keepdims=True) + RMS_EPS)
    return (y * g.astype(jnp.float32)).astype(x.dtype)


def t5_bucket(rel):
    half = T5_BUCKETS // 2
    max_exact = half // 2
    ret = jnp.where(rel > 0, half, 0)
    n = jnp.abs(rel)
    nf = jnp.maximum(n, 1).astype(jnp.float32)
    large = max_exact + (jnp.log(nf / max_exact) / math.log(T5_MAX_DIST / max_exact)
                         * (half - max_exact)).astype(jnp.int32)
    large = jnp.minimum(large, half - 1)
    return ret + jnp.where(n < max_exact, n, large)


def neighbourhood_attention(q, k, v, rpb):
    B, S, H, dh = q.shape
    rows = S // GRID_W
    wr = min(NA_WIN_ROWS, rows)
    q = q.reshape(B, rows, GRID_W, H, dh)
    k = k.reshape(B, rows, GRID_W, H, dh)
    v = v.reshape(B, rows, GRID_W, H, dh)
    r = jnp.arange(rows)
    r0 = jnp.clip(r - wr // 2, 0, rows - wr)
    row_idx = r0[:, None] + jnp.arange(wr)[None, :]
    kg = k[:, row_idx].reshape(B, rows, wr * GRID_W, H, dh)
    vg = v[:, row_idx].reshape(B, rows, wr * GRID_W, H, dh)
    c = jnp.arange(GRID_W)
    c0 = jnp.clip(c - NA_WIN_COLS // 2, 0, GRID_W - NA_WIN_COLS)
    kc = c[None, :]
    valid = (kc >= c0[:, None]) & (kc < c0[:, None] + NA_WIN_COLS)
    mask = jnp.tile(valid, (1, wr))
    dr = row_idx - r[:, None]
    dc = jnp.clip(kc - c[:, None], -(NA_WIN_COLS - 1), NA_WIN_COLS - 1)
    bias = rpb[:, (dr + NA_WIN_ROWS - 1)[:, :, None, None],
               (dc + NA_WIN_COLS - 1)[None, None, :, :]]
    bias = bias.transpose(0, 1, 3, 2, 4).reshape(H, rows, GRID_W, wr * GRID_W)
    scale = 1.0 / math.sqrt(dh)
    logits = jnp.einsum('brqhd,brkhd->bhrqk', q, kg).astype(jnp.float32) * scale
    logits = logits + bias.astype(jnp.float32)[None]
    logits = jnp.where(mask, logits, NEG_INF)
    p = jax.nn.softmax(logits, axis=-1).astype(v.dtype)
    out = jnp.einsum('bhrqk,brkhd->brqhd', p, vg)
    return out.reshape(B, S, H * dh)


def differential_attention(q1, q2, k1, k2, v, t5_table, lam):
    B, S, H, dk = q1.shape
    nblk = S // Q_BLOCK
    scale = 1.0 / math.sqrt(dk)
    qs = jnp.stack([q1, q2], axis=0).reshape(2, B, nblk, Q_BLOCK, H, dk)
    qs = jnp.moveaxis(qs, 2, 0)
    starts = jnp.arange(nblk, dtype=jnp.int32) * Q_BLOCK
    key_pos = jnp.arange(S, dtype=jnp.int32)

    def block(args):
        qb, start = args
        qpos = start + jnp.arange(Q_BLOCK, dtype=jnp.int32)
        bucket = t5_bucket(key_pos[None, :] - qpos[:, None])
        bias = jnp.transpose(t5_table[bucket], (2, 0, 1)).astype(jnp.float32)
        s1 = jnp.einsum('bqhd,bkhd->bhqk', qb[0], k1).astype(jnp.float32) * scale + bias
        s2 = jnp.einsum('bqhd,bkhd->bhqk', qb[1], k2).astype(jnp.float32) * scale + bias
        p = (jax.nn.softmax(s1, axis=-1) - lam * jax.nn.softmax(s2, axis=-1)).astype(v.dtype)
        return jnp.einsum('bhqk,bkhd->bqhd', p, v)

    out = lax.map(block, (qs, starts))
    return jnp.moveaxis(out, 0, 1).reshape(B, S, H, v.shape[-1])


def setup_inputs(seed: int = 0) -> dict:
    key = jax.random.key(seed)
    ks = jax.random.split(key, 16)
    f32 = jnp.float32

    def nrm(k, shape, s):
        return jax.random.normal(k, shape, f32) * s

    return {
        "x": nrm(ks[0], (BATCH, SEQ, D_MODEL), 1.0),
        "t5_bias": nrm(ks[1], (T5_BUCKETS, DIFF_HEADS), 0.5),
        "final_norm_g": 1.0 + nrm(ks[2], (D_MODEL,), 0.02),
        "norm1_g": 1.0 + nrm(ks[3], (DEPTH, D_MODEL), 0.02),
        "w_in": nrm(ks[4], (DEPTH, D_MODEL, W_IN_COLS), D_MODEL ** -0.5),
        "na_rpb": nrm(ks[5], (DEPTH, NA_HEADS, 2 * NA_WIN_ROWS - 1, 2 * NA_WIN_COLS - 1), 0.2),
        "diff_lambda": nrm(ks[6], (DEPTH, 4, DIFF_QK_DIM), 0.1),
        "diff_subln_g": 1.0 + nrm(ks[7], (DEPTH, DIFF_V_DIM), 0.02),
        "w_na_o": nrm(ks[8], (DEPTH, NA_WIDTH, D_MODEL), NA_WIDTH ** -0.5),
        "w_diff_o": nrm(ks[9], (DEPTH, DIFF_V_WIDTH, D_MODEL), DIFF_V_WIDTH ** -0.5),
        "w_out": nrm(ks[10], (DEPTH, D_MODEL, D_MODEL), D_MODEL ** -0.5),
        "norm2_g": 1.0 + nrm(ks[11], (DEPTH, D_MODEL), 0.02),
        "w_ff1": nrm(ks[12], (DEPTH, D_MODEL, D_FF), D_MODEL ** -0.5),
        "w_ff2": nrm(ks[13], (DEPTH, D_FF, D_MODEL), D_FF ** -0.5),
    }


def reference(x, t5_bias, final_norm_g, norm1_g, w_in, na_rpb, diff_lambda, diff_subln_g,
              w_na_o, w_diff_o, w_out, norm2_g, w_ff1, w_ff2):
    B, S, D = x.shape
    splits = np.cumsum([NA_WIDTH, NA_WIDTH, NA_WIDTH, DIFF_QK_WIDTH, DIFF_QK_WIDTH,
                        DIFF_V_WIDTH, D_MODEL]).tolist()
    for layer in range(DEPTH):
        h = rmsnorm(x, norm1_g[layer])
        proj = jnp.einsum('bsd,de->bse', h, w_in[layer])
        qa, ka, va, qd, kd, vd, ga, gd = jnp.split(proj, splits, axis=-1)

        shp_a = (B, S, NA_HEADS, NA_HEAD_DIM)
        y_na = neighbourhood_attention(qa.reshape(shp_a), ka.reshape(shp_a),
                                       va.reshape(shp_a), na_rpb[layer])

        qd = qd.reshape(B, S, DIFF_HEADS, 2, DIFF_QK_DIM)
        kd = kd.reshape(B, S, DIFF_HEADS, 2, DIFF_QK_DIM)
        vd = vd.reshape(B, S, DIFF_HEADS, DIFF_V_DIM)
        lam_init = 0.8 - 0.6 * math.exp(-0.3 * layer)
        lp = diff_lambda[layer].astype(jnp.float32)
        lam = (jnp.exp(jnp.sum(lp[0] * lp[1])) - jnp.exp(jnp.sum(lp[2] * lp[3]))
               + lam_init)
        od = differential_attention(qd[..., 0, :], qd[..., 1, :], kd[..., 0, :],
                                    kd[..., 1, :], vd, t5_bias, lam)
        od = rmsnorm(od, diff_subln_g[layer]) * (1.0 - lam_init)
        y_diff = od.reshape(B, S, DIFF_V_WIDTH)

        b_na = jnp.einsum('bse,ed->bsd', y_na, w_na_o[layer])
        b_diff = jnp.einsum('bse,ed->bsd', y_diff, w_diff_o[layer])
        merged = jax.nn.sigmoid(ga) * b_na + jax.nn.sigmoid(gd) * b_diff
        x = x + jnp.einsum('bsd,de->bse', merged, w_out[layer])

        h2 = rmsnorm(x, norm2_g[layer])
        u = jnp.square(jax.nn.relu(jnp.einsum('bsd,df->bsf', h2, w_ff1[layer])))
        x = x + jnp.einsum('bsf,fd->bsd', u, w_ff2[layer])
    return rmsnorm(x, final_norm_g)
```

```python
import math
import contextlib
import numpy as np
import concourse.bass as bass
import concourse.mybir as mybir
from concourse.bass_utils import run_bass_kernel_spmd

F32 = mybir.dt.float32
BF16 = mybir.dt.bfloat16
AF = mybir.ActivationFunctionType
ALU = mybir.AluOpType
AX = mybir.AxisListType

ENGS = ("pe", "act", "dve", "pool", "sp")
NEG = -30000.0
PAIR_EXP = False
HOIST_PE_WAITS = True
EPS = 1e-6


class Op:
    __slots__ = ("eng", "fn", "deps", "dma", "idx", "signal", "sigval", "dmaval", "gseq", "hoist")

    def __init__(self, eng, fn, dma):
        self.gseq = 0
        self.hoist = False
        self.eng = eng
        self.fn = fn
        self.deps = set()
        self.dma = dma
        self.idx = -1
        self.signal = False
        self.sigval = 0
        self.dmaval = 0


class Prog:
    def __init__(self, nc):
        self.nc = nc
        self.ops = {e: [] for e in ENGS}
        self.lastw = {}
        self.readers = {}
        self.dma_counts = {}
        self.dma_last = {}
        self.pending = {e: None for e in ENGS}
        self.gcount = 0

    def barrier(self):
        deps = set()
        for e in ENGS:
            for o in reversed(self.ops[e]):
                if o.dma is None:
                    deps.add(o)
                    break
        for k, o in self.dma_last.items():
            deps.add(o)
        for e in ENGS:
            self.pending[e] = set(deps) | (self.pending[e] or set())

    def op(self, eng, fn, reads=(), writes=(), dma=None, hoist=False):
        o = Op(eng, fn, dma)
        o.hoist = hoist
        self.gcount += 1
        o.gseq = self.gcount
        o.idx = len(self.ops[eng])
        raw = set()
        deps = set()
        for r in reads:
            w = self.lastw.get(r)
            if w is not None:
                deps.add(w)
                raw.add(w)
        for k in writes:
            w = self.lastw.get(k)
            if w is not None:
                deps.add(w)
            for rd in self.readers.get(k, ()):
                deps.add(rd)
        keep = set()
        for d in deps:
            if d is o:
                continue
            if d.eng == eng and d.dma is None:
                if eng == "pe":
                    continue
                if d in raw and (o.idx - d.idx) <= 3:
                    keep.add(d)
                continue
            keep.add(d)
        if self.pending[eng]:
            for d in self.pending[eng]:
                if d.eng == eng and d.dma is None:
                    continue
                keep.add(d)
            self.pending[eng] = None
        o.deps = keep
        for d in keep:
            if d.dma is None:
                d.signal = True
        for r in reads:
            self.readers.setdefault(r, []).append(o)
        for k in writes:
            self.lastw[k] = o
            self.readers[k] = []
        if dma is not None:
            c = self.dma_counts.get(dma, 0) + 16
            self.dma_counts[dma] = c
            o.dmaval = c
            self.dma_last[dma] = o
        self.ops[eng].append(o)
        return o

    def emit(self, final_waits=()):
        nc = self.nc
        with contextlib.ExitStack() as st:
            esem = {e: st.enter_context(nc.semaphore("s_" + e)) for e in ENGS}
            dsem = {}
            for i, k in enumerate(self.dma_counts):
                dsem[k] = st.enter_context(nc.semaphore("d%d" % i))
            for e in ENGS:
                c = 0
                for o in self.ops[e]:
                    if o.signal:
                        c += 1
                        o.sigval = c
            block = st.enter_context(nc.Block())
            ops = self.ops
            dma_counts = self.dma_counts

            def run(eng, e):
                waited = {}
                needs = []
                for o in ops[e]:
                    need = {}
                    for d in o.deps:
                        if d.dma is not None:
                            key = ("d", d.dma)
                            val = d.dmaval
                            sem = dsem[d.dma]
                        else:
                            key = ("e", d.eng)
                            val = d.sigval
                            sem = esem[d.eng]
                        if val > need.get(key, (None, 0))[1]:
                            need[key] = (sem, val)
                    needs.append(need)
                if e == "pe" and HOIST_PE_WAITS:
                    lst = ops[e]
                    for j in range(1, len(lst)):
                        y = lst[j - 1]
                        if lst[j].hoist and needs[j] and all(d.gseq < y.gseq for d in lst[j].deps):
                            for key, (sem, val) in needs[j].items():
                                if val > needs[j - 1].get(key, (None, 0))[1]:
                                    needs[j - 1][key] = (sem, val)
                            needs[j] = {}
                for o, need in zip(ops[e], needs):
                    for key, (sem, val) in need.items():
                        if waited.get(key, 0) >= val:
                            continue
                        waited[key] = val
                        eng.wait_ge(sem, val)
                    inst = o.fn(eng)
                    if o.dma is not None:
                        inst.then_inc(dsem[o.dma], 16)
                    elif o.signal:
                        inst.then_inc(esem[e], 1)
                if e == "sp":
                    for k in final_waits:
                        eng.wait_ge(dsem[k], dma_counts[k])

            @block.tensor
            def _(eng):
                run(eng, "pe")

            @block.scalar
            def _(eng):
                run(eng, "act")

            @block.vector
            def _(eng):
                run(eng, "dve")

            @block.gpsimd
            def _(eng):
                run(eng, "pool")

            @block.sync
            def _(eng):
                run(eng, "sp")


def view(base, off_bytes, shape, dtype):
    n = int(np.prod(shape))
    esz = 2 if dtype == BF16 else 4
    assert off_bytes % 4 == 0
    a = base[:, off_bytes // 2: off_bytes // 2 + n * esz // 2]
    if dtype == F32:
        a = a.bitcast(F32)
    if len(shape) == 2:
        a = a.rearrange("p (a b) -> p a b", b=shape[1])
    elif len(shape) == 3:
        a = a.rearrange("p (a b c) -> p a b c", b=shape[1], c=shape[2])
    return a


def bcast_mid(ap, count):
    a = [list(t) for t in ap.ap]
    return bass.AP(ap.tensor, ap.offset, [a[0], [0, count]] + a[1:])


def bcast_last(ap, count):
    a = [list(t) for t in ap.ap]
    return bass.AP(ap.tensor, ap.offset, a + [[0, count]])


def mm1(out, lhsT, rhs, start=True, stop=True):
    return lambda e: e.matmul(out, lhsT, rhs, start=start, stop=stop)


def mmk(out, pairs):
    def f(e):
        n = len(pairs)
        ins = None
        for i, (a, b) in enumerate(pairs):
            ins = e.matmul(out, a, b, start=(i == 0), stop=(i == n - 1))
        return ins
    return f


def mmlist(items):
    def f(e):
        ins = None
        for (o, a, b, s0, s1) in items:
            ins = e.matmul(o, a, b, start=s0, stop=s1)
        return ins
    return f


def trlist(items, ident):
    def f(e):
        ins = None
        for (o, a) in items:
            ins = e.transpose(o, a, ident)
        return ins
    return f


def dma(out, in_):
    return lambda e: e.dma_start(out=out, in_=in_)


def _t5_bucket_np(rel, mode):
    rel = rel.astype(np.int32)
    ret = np.where(rel > 0, 16, 0)
    n = np.abs(rel)
    nf = np.maximum(n, 1).astype(np.float32)
    v = np.log(nf / np.float32(8)) / np.float32(math.log(16)) * np.float32(8)
    if mode == "round":
        large = 8 + np.round(v).astype(np.int32)
    else:
        large = 8 + v.astype(np.int32)
    large = np.minimum(large, 15)
    return ret + np.where(n < 8, n, large)


T5_MODE = "trunc"


def _na_tile_list():
    tiles = [(5, 5 + d) for d in (-2, -1, 0, 1, 2)]
    tiles += [(0, m) for m in range(4)] + [(1, m) for m in range(4)]
    tiles += [(14, m) for m in range(12, 16)] + [(15, m) for m in range(12, 16)]
    return tiles


def _host_tables(t5_bias, na_rpb):
    kl = np.arange(128)[:, None]
    j = np.arange(384)[None, :]
    bucket = _t5_bucket_np(kl + 128 - j, T5_MODE)
    t5band = np.ascontiguousarray(np.transpose(t5_bias[bucket], (0, 2, 1))).astype(np.float32)
    t5c = np.ascontiguousarray(t5_bias[[15, 31], :]).astype(np.float32).reshape(8)
    tiles = _na_tile_list()
    k_l = np.arange(128)[:, None]
    q_l = np.arange(128)[None, :]
    a, kc = k_l // 64, k_l % 64
    b, qc = q_l // 64, q_l % 64
    dr_idx = np.zeros((21, 128, 128), np.int64)
    dc_idx = np.zeros((21, 128, 128), np.int64)
    valid = np.zeros((21, 128, 128), bool)
    for t, (tc, m) in enumerate(tiles):
        r = 2 * tc + b
        krow = 2 * m + a
        r0 = np.clip(r - 4, 0, 24)
        vrow = (krow >= r0) & (krow <= r0 + 7)
        c0 = np.clip(qc - 8, 0, 48)
        vcol = (kc >= c0) & (kc < c0 + 16)
        dr = krow - r
        dc = np.clip(kc - qc, -15, 15)
        v = vrow & vcol
        valid[t] = v
        dr_idx[t] = np.where(v, dr + 7, 0)
        dc_idx[t] = np.where(v, dc + 15, 0)
    g = na_rpb[:, :, dr_idx, dc_idx]
    g = np.where(valid[None, None], g, np.float32(NEG)).astype(np.float32)
    nab = np.ascontiguousarray(np.transpose(g, (0, 1, 3, 2, 4)))
    return t5band, t5c, nab


def build(layers, final, dbg=None):
    nc = bass.Bass("TRN2", target_bir_lowering=False)
    dt_in = lambda name, shape: nc.dram_tensor(name, shape, F32, kind="ExternalInput").ap()
    x_d = dt_in("x", [2048, 1024])
    w_in_d = dt_in("w_in", [2, 1024, 5120])
    w_na_o_d = dt_in("w_na_o", [2, 512, 1024])
    w_diff_o_d = dt_in("w_diff_o", [2, 512, 1024])
    w_out_d = dt_in("w_out", [2, 1024, 1024])
    w_ff1_d = dt_in("w_ff1", [2, 1024, 4096])
    w_ff2_d = dt_in("w_ff2", [2, 4096, 1024])
    n1g_d = dt_in("norm1_g", [2, 1024])
    n2g_d = dt_in("norm2_g", [2, 1024])
    fg_d = dt_in("final_norm_g", [1024])
    lam_d = dt_in("diff_lambda", [512])
    subg_d = dt_in("diff_subln_g", [256])
    t5c_d = dt_in("t5c", [8])
    t5band_d = dt_in("t5band", [128, 4, 384])
    nab_d = dt_in("nab", [2, 8, 128, 21, 128])
    ident_d = dt_in("ident", [128, 128])
    out_d = nc.dram_tensor("out", [2048, 1024], F32, kind="ExternalOutput").ap()

    xs = nc.alloc_sbuf_tensor("xs", [128, 16, 1024], F32)
    hT = nc.alloc_sbuf_tensor("hT", [128, 8, 2048], BF16)
    yTf = nc.alloc_sbuf_tensor("yTf", [128, 16384], BF16)
    ident = nc.alloc_sbuf_tensor("ident_sb", [128, 128], BF16)
    gbc = nc.alloc_sbuf_tensor("gbc", [128, 1024], F32)
    cst = nc.alloc_sbuf_tensor("cst", [128, 32], F32)
    gsub = nc.alloc_sbuf_tensor("gsub", [128, 2, 128], F32)
    stat = nc.alloc_sbuf_tensor("stat", [128, 64], F32)
    junk2 = nc.alloc_sbuf_tensor("junk2", [128, 128], BF16)
    junk3 = nc.alloc_sbuf_tensor("junk3", [128, 128], BF16)
    hsb = nc.alloc_sbuf_tensor("hsb", [128, 2048], BF16)
    hs = [hsb[:, 0:1024], hsb[:, 1024:2048]]
    kTb = hsb[:, :]
    DYN = 69760
    dyn = nc.alloc_sbuf_tensor("dyn", [128, DYN // 2], BF16)
    assert nc.sbuf_bytes_remaining >= 0

    yT = yTf[:, :].rearrange("p (c t) -> p c t", t=2048)
    w1g = [view(yTf, s * 16384 + 0, [8, 512], BF16) for s in range(2)]
    w2g = [view(yTf, s * 16384 + 8192, [4, 1024], BF16) for s in range(2)]
    wo = view(yTf, 16384, [8, 1024], BF16)

    wslot = [view(dyn, s * 6144, [3, 8, 128], BF16) for s in range(2)]
    wslotC = [view(dyn, s * 6144, [24, 128], BF16) for s in range(2)]
    bI = view(dyn, 12288, [5, 128], F32)
    bEt = view(dyn, 14848, [8, 128], F32)
    bEb = view(dyn, 18944, [8, 128], F32)
    A0 = 23040
    qT = view(dyn, A0, [2048], BF16)
    kT = view(dyn, A0 + 4096, [2048], BF16)
    vv = [view(dyn, A0 + 8192 + s * 4160, [16, 130], BF16) for s in range(2)]
    PTn = [view(dyn, A0 + 16512 + s * 1280, [640], BF16) for s in range(3)]
    PTd = [view(dyn, A0 + 20352 + s * 1024, [2, 256], BF16) for s in range(4)]
    PTd2 = [view(dyn, A0 + 20352 + s * 2048, [1024], BF16) for s in range(2)]
    ytok = view(dyn, A0 + 28288, [16, 128], BF16)
    dband = view(dyn, A0 + 32384, [4, 384], F32)
    od = view(dyn, A0 + 24448, [4, 128], F32)
    tq = view(dyn, A0 + 26496, [128], F32)
    qTxb = view(dyn, 12288, [8, 512], BF16)
    qTx = view(dyn, 61568, [8, 512], BF16)
    lamb = hs[0].bitcast(F32)
    subgb = hs[1][:, 0:512].bitcast(F32)
    merged = view(dyn, A0, [8, 2048], BF16)
    sa = [view(dyn, 12288 + s * 4096, [512], F32) for s in range(2)]
    sd = [view(dyn, 12288 + 2048 + s * 4096, [512], F32) for s in range(2)]
    uT = [view(dyn, A0 + s * 16384, [4, 2048], BF16) for s in range(2)]
    rl = [view(dyn, A0 + 32768 + s * 2048, [512], F32) for s in range(2)]
    of32 = [view(dyn, s * 4096, [1024], F32) for s in range(2)]

    psp = [nc.alloc_psum_tensor("psp%d" % i, [128, 2, 512], F32) for i in range(4)]

    def bank(i):
        return psp[i // 2][:, i % 2, :]

    def bank_bf(i):
        return psp[i // 2].bitcast(BF16)[:, i % 2, :]

    def pflat(s, n):
        return bass.AP(psp[s], 0, [[1024, 128], [1, n]])

    P = Prog(nc)
    pk = lambda i: ("p", i)

    P.op("sp", dma(cst[:, 0:8], t5c_d.partition_broadcast(128)), writes=["cst"], dma="cst")
    P.op("sp", dma(lamb, lam_d.partition_broadcast(128)), writes=["lamb"], dma="lamb")
    P.op("sp", dma(subgb, subg_d.partition_broadcast(128)), writes=["subgb"], dma="subgb")
    xv = x_d.rearrange("(t p) d -> p t d", p=128)
    for g in range(8):
        P.op("sp", dma(xs[:, g * 2:(g + 1) * 2, :], xv[:, g * 2:(g + 1) * 2, :]),
             writes=[("x", g * 2 + i) for i in range(2)], dma=("x", g))
    P.op("pool", dma(ident[:], ident_d), writes=["ident"], dma="ident")
    P.op("dve", lambda e: e.memset(qTx[0:64, :, 256:512], 0.0), writes=["qTx0"])
    P.op("dve", lambda e: e.memset(qTx[64:128, :, 0:256], 0.0), writes=["qTx0"])
    lam_init = [0.8 - 0.6 * math.exp(-0.3 * l) for l in range(2)]
    for l in range(2):
        for j in range(2):
            a0 = l * 256 + j * 128
            P.op("dve", (lambda a0=a0: lambda e: e.tensor_tensor(gbc[:, 0:64], lamb[:, a0:a0 + 64], lamb[:, a0 + 64:a0 + 128], ALU.mult))(),
                 reads=["lamb"], writes=["gbc"])
            col = 8 + l * 4 + j
            P.op("dve", (lambda col=col: lambda e: e.reduce_sum(cst[:, col:col + 1], gbc[:, 0:64], axis=AX.X))(),
                 reads=["gbc"], writes=[("cc", col)])
            P.op("act", (lambda col=col: lambda e: e.activation(cst[:, col:col + 1], cst[:, col:col + 1], AF.Exp))(),
                 reads=[("cc", col)], writes=[("cc", col)])
        c0 = 8 + l * 4
        P.op("dve", (lambda c0=c0: lambda e: e.tensor_tensor(cst[:, c0 + 2:c0 + 3], cst[:, c0:c0 + 1], cst[:, c0 + 1:c0 + 2], ALU.subtract))(),
             reads=[("cc", c0), ("cc", c0 + 1)], writes=[("cc", c0 + 2)])
        P.op("dve", (lambda c0=c0, li=lam_init[l]: lambda e: e.tensor_scalar(cst[:, c0 + 3:c0 + 4], cst[:, c0 + 2:c0 + 3], li, -1.0, ALU.add, ALU.mult))(),
             reads=[("cc", c0 + 2)], writes=[("neglam", l)])
        P.op("dve", (lambda l=l: lambda e: e.tensor_scalar_mul(gsub[:, l, :], subgb[:, l * 128:(l + 1) * 128], 1.0 - lam_init[l]))(),
             reads=["subgb"], writes=[("gsub", l)])
    neglam = [cst[:, 8 + l * 4 + 3: 8 + l * 4 + 4] for l in range(2)]

    trn = [0]

    def norm_phase(g_row, is_final, hooked=False):
        P.op("sp", dma(gbc[:], g_row.partition_broadcast(128)), writes=["gbc"], dma="gbc")
        sq_out = bass.AP(junk2, 0, [[128, 128], [0, 8], [1, 128]])

        def stats(grp):
            for j in range(4):
                tc = grp * 4 + j
                P.op("act", (lambda tc=tc: lambda e: e.activation(sq_out, xs[:, tc, :].rearrange("p (a b) -> p a b", a=8), AF.Square,
                                                                   scale=1.0 / 32, accum_out=stat[:, tc:tc + 1]))(),
                     reads=[("x", tc)], writes=["junk2", ("ss", tc)])
            P.op("act", (lambda grp=grp: lambda e: e.activation(stat[:, 16 + grp * 4:20 + grp * 4], stat[:, grp * 4:grp * 4 + 4], AF.Ln, bias=EPS))(),
                 reads=[("ss", grp * 4 + i) for i in range(4)], writes=[("ln", grp)])
            P.op("act", (lambda grp=grp: lambda e: e.activation(stat[:, 32 + grp * 4:36 + grp * 4], stat[:, 16 + grp * 4:20 + grp * 4], AF.Exp, scale=-0.5))(),
                 reads=[("ln", grp)], writes=[("rstd", grp)])

        def apply(grp):
            for j in range(4):
                tc = grp * 4 + j
                s = tc % 2
                if is_final:
                    P.op("dve", (lambda tc=tc, s=s: lambda e: e.scalar_tensor_tensor(of32[s], xs[:, tc, :], stat[:, 32 + tc:33 + tc], gbc[:], ALU.mult, ALU.mult))(),
                         reads=[("x", tc), ("rstd", grp), "gbc"], writes=[("of", s)] + [("w", ws_, bi_) for ws_ in range(2) for bi_ in range(4)])
                    P.op("sp", dma(out_d[tc * 128:(tc + 1) * 128, :], of32[s]), reads=[("of", s)], dma="out")
                    continue
                P.op("dve", (lambda tc=tc, s=s: lambda e: e.scalar_tensor_tensor(hs[s], xs[:, tc, :], stat[:, 32 + tc:33 + tc], gbc[:], ALU.mult, ALU.mult))(),
                     reads=[("x", tc), ("rstd", grp), "gbc"], writes=[("hs", s)])
                b = 6 + (trn[0] % 2)
                trn[0] += 1
                pb = bank_bf(b)
                P.op("pe", trlist([(pb[:, c * 128:(c + 1) * 128], hs[s][:, c * 128:(c + 1) * 128]) for c in range(8)], ident[:]),
                     reads=[("hs", s), "ident"], writes=[pk(b)])
                src = pb[:, 0:1024].rearrange("p (c t) -> p c t", t=128)
                dst = hT[:, :, tc * 128:(tc + 1) * 128]
                if tc % 2 == 0:
                    P.op("act", (lambda dst=dst, src=src: lambda e: e.activation(dst, src, AF.Copy))(),
                         reads=[pk(b)], writes=[("hT", tc)])
                else:
                    P.op("dve", (lambda dst=dst, src=src: lambda e: e.tensor_copy(dst, src))(),
                         reads=[pk(b)], writes=[("hT", tc)])

        if hooked:
            return stats, apply
        stats(0)
        for grp in range(4):
            if grp + 1 < 4:
                stats(grp + 1)
            apply(grp)

    def hT_keys(tb):
        return [("hT", tb * 4 + i) for i in range(4)]

    def unit_cols(u):
        if u < 4:
            return (u * 128, 512 + u * 128, 1024 + u * 128)
        h = u - 4
        return (1536 + h * 128, 2048 + h * 128, 2560 + h * 128)

    def load_unit_w(l, u, after_x=False):
        s = u % 2
        wv_ = w_in_d[l].rearrange("(dc p) e -> p dc e", p=128)
        rd = [("x", 15)] if after_x else []
        for bi, c0 in enumerate(unit_cols(u)):
            P.op("pool", dma(wslot[s][:, bi], wv_[:, :, c0:c0 + 128]), reads=rd, writes=[("w", s, bi)], dma=("w", s, bi))

    def load_na_bias(l, h, which, after_x=False):
        src = nab_d[l, h]
        rd = [("x", 15)] if after_x else []
        if which == "I":
            P.op("sp", dma(bI, src[:, 0:5, :]), reads=rd, writes=["bI"], dma="bI")
        elif which == "Et":
            P.op("sp", dma(bEt, src[:, 5:13, :]), reads=rd, writes=["bEt"], dma="bEt")
        else:
            P.op("sp", dma(bEb, src[:, 13:21, :]), reads=rd, writes=["bEb"], dma="bEb")

    pjn = [0]

    def proj_chunks(l, u, banks=None, dset=0):
        s = u % 2
        vs = u % 2
        is_na = u < 4
        wq, wk, wv_ = wslot[s][:, 0], wslot[s][:, 1], wslot[s][:, 2]
        kT_s = kT if dset == 0 else kTb
        qTx_s = qTx if dset == 0 else qTxb
        kkey = "kT" if dset == 0 else "kTb"
        hsk = [] if dset == 0 else [("hs", 0), ("hs", 1)]
        z0 = "qTx0" if dset == 0 else "qTx0b"
        chunks = []

        cur = [None]

        def nb():
            if cur[0] is not None:
                return cur[0]
            b = pjn[0] % 6
            pjn[0] += 1
            return b

        def qk_chunk(which, wt, tb):
            b = nb()
            P.op("pe", mmk(bank(b), [(wt[:, dc, :], hT[:, dc, tb * 512:(tb + 1) * 512]) for dc in range(8)]),
                 reads=[("w", s, which)] + hT_keys(tb), writes=[pk(b)])
            if (not is_na) and which == 0:
                d0 = qTx_s[0:64, 2 * tb:2 * tb + 2, 0:256]
                d1 = qTx_s[64:128, 2 * tb:2 * tb + 2, 256:512]
                s0 = bank(b)[0:64, :].rearrange("p (a c) -> p a c", a=2)
                s1 = bank(b)[64:128, :].rearrange("p (a c) -> p a c", a=2)
                P.op("dve", (lambda d0=d0, s0=s0: lambda e: e.tensor_scalar_mul(d0, s0, 0.125))(),
                     reads=[pk(b), z0], writes=[("qTa", dset, tb)])
                P.op("dve", (lambda d1=d1, s1=s1: lambda e: e.tensor_scalar_mul(d1, s1, 0.125))(),
                     reads=[pk(b), z0], writes=[("qTb", dset, tb)])
                return
            if which == 0:
                d0 = qTx[0:64, 2 * tb:2 * tb + 2, 0:256]
                d1 = qTx[64:128, 2 * tb:2 * tb + 2, 256:512]
                s0 = bank(b)[0:64, :].rearrange("p (a c) -> p a c", a=2)
                s1 = bank(b)[64:128, :].rearrange("p (a c) -> p a c", a=2)
                P.op("act", (lambda d0=d0, s0=s0: lambda e: e.activation(d0, s0, AF.Copy, scale=0.125))(),
                     reads=[pk(b), "qTx0"], writes=[("qTa", 0, tb)])
                P.op("act", (lambda d1=d1, s1=s1: lambda e: e.activation(d1, s1, AF.Copy, scale=0.125))(),
                     reads=[pk(b), "qTx0"], writes=[("qTb", 0, tb)])
            elif is_na:
                dst = kT[:, tb * 512:(tb + 1) * 512]
                P.op("act", (lambda dst=dst, b=b: lambda e: e.activation(dst, bank(b), AF.Copy))(),
                     reads=[pk(b)], writes=[("kT", tb)])
            else:
                dst = kT_s[:, tb * 512:(tb + 1) * 512]
                P.op("dve", (lambda dst=dst, b=b: lambda e: e.tensor_copy(dst, bank(b)))(),
                     reads=[pk(b)], writes=[(kkey, tb)] + hsk)

        def v_chunk(tg):
            if tg == 0:
                if is_na:
                    ones_ap = vv[vs][:, :, :].rearrange("p t (h e) -> p t h e", h=2)[:, :, :, 64:65]
                else:
                    ones_ap = vv[vs][:, :, 128:129]
                P.op("pool", (lambda ones_ap=ones_ap: lambda e: e.memset(ones_ap, 1.0))(), writes=[("v1", vs)])
            b = nb()
            items = []
            for j in range(4):
                tc = tg * 4 + j
                for dc in range(8):
                    items.append((bank(b)[:, j * 128:(j + 1) * 128], hT[:, dc, tc * 128:(tc + 1) * 128], wv_[:, dc, :], dc == 0, dc == 7))
            P.op("pe", mmlist(items), reads=[("w", s, 2)] + hT_keys(tg), writes=[pk(b)])
            if is_na:
                dst = vv[vs][:, tg * 4:(tg + 1) * 4, :].rearrange("p t (h e) -> p t h e", h=2)[:, :, :, 0:64]
                src = bank(b).rearrange("p (t h e) -> p t h e", t=4, h=2)
                P.op("act", (lambda dst=dst, src=src: lambda e: e.activation(dst, src, AF.Copy))(),
                     reads=[pk(b)], writes=[("v", vs, tg)])
            else:
                dst = vv[vs][:, tg * 4:(tg + 1) * 4, 0:128]
                src = bank(b).rearrange("p (t e) -> p t e", t=4)
                P.op("dve", (lambda dst=dst, src=src: lambda e: e.tensor_copy(dst, src))(),
                     reads=[pk(b)], writes=[("v", vs, tg)])

        def wrap(f):
            def g(bk=None):
                cur[0] = bk
                f()
                cur[0] = None
            return g

        for which, wt in ((0, wq), (1, wk)):
            for tb in range(4):
                chunks.append(wrap((lambda which=which, wt=wt, tb=tb: lambda: qk_chunk(which, wt, tb))()))
        for tg in range(4):
            chunks.append(wrap((lambda tg=tg: lambda: v_chunk(tg))()))
        return chunks

    def projection(l, u):
        for c in proj_chunks(l, u):
            c()

    def y_transposes(u):
        for half in range(2):
            b = 7 if half == 0 else (5 if u >= 4 else 6)
            pb = bank_bf(b)
            P.op("pe", trlist([(pb[:, j * 128:(j + 1) * 128], ytok[:, half * 8 + j, :]) for j in range(8)], ident[:]),
                 reads=[("ytok", half * 8 + j) for j in range(8)] + ["ident"], writes=[pk(b)])
            dst = yT[:, u, half * 1024:(half + 1) * 1024]
            if half == 0:
                P.op("act", (lambda dst=dst, pb=pb: lambda e: e.activation(dst, pb[:, 0:1024], AF.Copy))(),
                     reads=[pk(b)], writes=[("yT", u)])
            else:
                P.op("dve", (lambda dst=dst, pb=pb: lambda e: e.tensor_copy(dst, pb[:, 0:1024]))(),
                     reads=[pk(b)], writes=[("yTb", u)])

    def na_chunks(tc):
        if tc <= 1:
            return [0, 1, 2, 3]
        if tc >= 14:
            return [12, 13, 14, 15]
        return [tc - 2, tc - 1, tc, tc + 1, tc + 2]

    def na_attention(l, u, next_bias):
        vs = u % 2
        obank = [6, 7]
        steps = [(hh, tc) for hh in range(2) for tc in range(16)]
        NS = 3

        def QK(i):
            hh, tc = steps[i]
            p0 = hh * 64
            s = i % NS
            ms = na_chunks(tc)
            items = []
            qsl = qTx[:, tc // 2, hh * 256 + (tc % 2) * 128: hh * 256 + (tc % 2) * 128 + 128]
            for j, m in enumerate(ms):
                items.append((pflat(s, 640)[:, j * 128:(j + 1) * 128], kT[:, m * 128:(m + 1) * 128], qsl, True, True))
            P.op("pe", mmlist(items), reads=[("qTa", 0, tc // 4), ("qTb", 0, tc // 4)] + sorted(set(("kT", m // 4) for m in ms)),
                 writes=[pk(2 * s), pk(2 * s + 1)])

        def ADDEXP(i):
            hh, tc = steps[i]
            s = i % NS
            n = len(na_chunks(tc)) * 128
            if tc == 0:
                bt, bkey = bEt[:, 0:4, :], "bEt"
            elif tc == 1:
                bt, bkey = bEt[:, 4:8, :], "bEt"
            elif tc == 14:
                bt, bkey = bEb[:, 0:4, :], "bEb"
            elif tc == 15:
                bt, bkey = bEb[:, 4:8, :], "bEb"
            else:
                bt, bkey = bI[:, 0:5, :], "bI"
            btf = bt.rearrange("p a b -> p (a b)")
            P.op("dve", (lambda s=s, n=n, btf=btf: lambda e: e.tensor_tensor(pflat(s, n), pflat(s, n), btf, ALU.add))(),
                 reads=[pk(2 * s), pk(2 * s + 1), bkey], writes=[pk(2 * s), pk(2 * s + 1)])
            P.op("act", (lambda s=s, n=n: lambda e: e.activation(PTn[s][:, 0:n], pflat(s, n), AF.Exp))(),
                 reads=[pk(2 * s), pk(2 * s + 1)], writes=[("PTn", s)])

        def PV(i):
            hh, tc = steps[i]
            s = i % NS
            ms = na_chunks(tc)
            ob = obank[(tc // 4) % 2]
            o_ap = bank(ob)[:, (tc % 4) * 65:(tc % 4) * 65 + 65]
            items = []
            for j, m in enumerate(ms):
                items.append((o_ap, PTn[s][:, j * 128:(j + 1) * 128], vv[vs][:, m, hh * 65:hh * 65 + 65], j == 0, j == len(ms) - 1))
            P.op("pe", mmlist(items), reads=[("PTn", s), ("v1", vs)] + sorted(set(("v", vs, m // 4) for m in ms)),
                 writes=[pk(ob)])

        def NORM(hh, g):
            ob = obank[g % 2]
            o3 = bank(ob)[:, 0:260].rearrange("p (t e) -> p t e", e=65)
            rec = stat[:, 48:52]
            P.op("dve", (lambda o3=o3, rec=rec: lambda e: e.reciprocal(rec, o3[:, :, 64]))(),
                 reads=[pk(ob)], writes=["rec"])
            dst = ytok[:, g * 4:(g + 1) * 4, hh * 64:(hh + 1) * 64]
            P.op("dve", (lambda o3=o3, rec=rec, dst=dst: lambda e: e.tensor_tensor(dst, o3[:, :, 0:64], bcast_last(rec, 64), ALU.mult))(),
                 reads=[pk(ob), "rec"], writes=[("ytok", g * 4 + i_) for i_ in range(4)])

        pro = NS - 1
        n = len(steps)
        for i in range(pro):
            QK(i)
            ADDEXP(i)
        for i in range(n):
            if i + pro < n:
                QK(i + pro)
                ADDEXP(i + pro)
            PV(i)
            hh, tc = steps[i]
            if tc % 4 == 1 and i >= 2:
                ph, pt = steps[i - 2]
                NORM(ph, pt // 4)
            nb = next_bias(2 * u + hh)
            if nb is not None:
                if tc == 2:
                    load_na_bias(nb[0], nb[1], "Et")
                elif tc == 11:
                    load_na_bias(nb[0], nb[1], "I")
                elif tc == 13:
                    load_na_bias(nb[0], nb[1], "Eb")
        NORM(1, 3)
        y_transposes(u)

    def diff_attention(l, h, inject=()):
        u = 4 + h
        dset = h % 2
        kT_s = kT if dset == 0 else kTb
        qTx_s = qTx if dset == 0 else qTxb
        kkey = "kT" if dset == 0 else "kTb"
        hsk = [] if dset == 0 else [("hs", 0), ("hs", 1)]
        vs = u % 2
        c15 = cst[:, h:h + 1]
        c31 = cst[:, 4 + h:5 + h]
        steps = [(qb, kc) for qb in range(8) for kc in range(16)]
        NS = 4

        def oblock(st_, qi, mp):
            bi = qi * 2 + mp
            bk = 4 + 2 * st_ + bi // 3
            c = (bi % 3) * 129
            return bk, bank(bk)[:, c:c + 129]

        def S3(b):
            return bank(b).rearrange("p (m c) -> p m c", m=2)

        def QK(i):
            qb, kc = steps[i]
            b = i % NS
            items = [(bank(b), kT_s[:, kc * 128:(kc + 1) * 128], qTx_s[:, qb, :], True, True)]
            P.op("pe", mmlist(items), reads=[("qTa", dset, qb // 2), ("qTb", dset, qb // 2), (kkey, kc // 4)] + hsk, writes=[pk(b)])

        def BANDEXP(i):
            qb, kc = steps[i]
            b = i % NS
            qlo = max(kc - 1, 2 * qb)
            qhi = min(kc + 1, 2 * qb + 1)
            if qlo <= qhi:
                c0 = (qlo - 2 * qb) * 128
                n = (qhi - qlo + 1) * 128
                j0 = (qlo - (kc - 1)) * 128
                P.op("dve", (lambda b=b, c0=c0, n=n, j0=j0: lambda e: e.tensor_tensor(
                    S3(b)[:, :, c0:c0 + n], S3(b)[:, :, c0:c0 + n], bcast_mid(dband[:, h, j0:j0 + n], 2), ALU.add))(),
                    reads=[pk(b), ("dband", h)], writes=[pk(b)])
            def kind(j):
                qb_, kc_ = steps[j]
                return min(max(kc_ - 1 - 2 * qb_, 0), 2)

            ncl = kind(i) * 128
            if PAIR_EXP and i % 2 == 0 and i + 1 < len(steps) and kind(i) != 1 and kind(i + 1) == kind(i):
                return
            if PAIR_EXP and i % 2 == 1 and kind(i) != 1 and kind(i - 1) == kind(i):
                cb = c15 if ncl == 0 else c31
                j = (b - 1) // 2
                P.op("act", (lambda j=j, cb=cb: lambda e: e.activation(PTd2[j], pflat(j, 1024), AF.Exp, bias=cb))(),
                     reads=[pk(b - 1), pk(b), "cst"], writes=[("PTd", b - 1), ("PTd", b)])
                return
            if ncl == 0 or ncl == 256:
                cb = c15 if ncl == 0 else c31
                P.op("act", (lambda b=b, cb=cb: lambda e: e.activation(PTd[b].rearrange("p m c -> p (m c)"), bank(b), AF.Exp, bias=cb))(),
                     reads=[pk(b), "cst"], writes=[("PTd", b)])
            else:
                P.op("act", (lambda b=b: lambda e: e.activation(PTd[b][:, :, 0:128], S3(b)[:, :, 0:128], AF.Exp, bias=c31))(),
                     reads=[pk(b), "cst"], writes=[("PTd", b)])
                P.op("act", (lambda b=b: lambda e: e.activation(PTd[b][:, :, 128:256], S3(b)[:, :, 128:256], AF.Exp, bias=c15))(),
                     reads=[pk(b), "cst", ("PTd", b)], writes=[("PTd", b)])

        def PV(i):
            qb, kc = steps[i]
            b = i % NS
            st_ = qb % 2
            items = []
            wk = set()
            started = set()
            for mp in range(2):
                for qi in range(2):
                    bk, o_ap = oblock(st_, qi, mp)
                    wk.add(pk(bk))
                    s0 = (kc == 0) and (bk not in started)
                    started.add(bk)
                    items.append((o_ap, PTd[b][:, mp, qi * 128:(qi + 1) * 128], vv[vs][:, kc, 0:129], s0, kc == 15))
            P.op("pe", mmlist(items), reads=[("PTd", b), ("v1", vs), ("v", vs, kc // 4)], writes=sorted(wk))

        def END_A(qb):
            st_ = qb % 2
            for qi in range(2):
                bk1, o1 = oblock(st_, qi, 0)
                bk2, o2 = oblock(st_, qi, 1)
                r1 = stat[:, 52 + 2 * qi:53 + 2 * qi]
                r2 = stat[:, 53 + 2 * qi:54 + 2 * qi]
                P.op("dve", (lambda o1=o1, r1=r1: lambda e: e.reciprocal(r1, o1[:, 128:129]))(), reads=[pk(bk1)], writes=[("r1", qi)])
                P.op("dve", (lambda o2=o2, r2=r2: lambda e: e.reciprocal(r2, o2[:, 128:129]))(), reads=[pk(bk2)], writes=[("r2", qi)])
                P.op("dve", (lambda o2=o2, r2=r2: lambda e: e.tensor_scalar(tq, o2[:, 0:128], r2, neglam[l], ALU.mult, ALU.mult))(),
                     reads=[pk(bk2), ("r2", qi), ("neglam", l)], writes=["tq"])
                P.op("dve", (lambda o1=o1, r1=r1, qi=qi: lambda e: e.scalar_tensor_tensor(od[:, qi, :], o1[:, 0:128], r1, tq, ALU.mult, ALU.add))(),
                     reads=[pk(bk1), ("r1", qi), "tq"], writes=[("od", qi)])

        def END_B(qb):
            for qi in range(2):
                P.op("dve", (lambda qi=qi: lambda e: e.scalar_tensor_tensor(junk3[:], od[:, qi, :], 1.0 / 128.0, od[:, qi, :], ALU.mult, ALU.mult,
                                                                            accum_out=stat[:, 60 + qi:61 + qi]))(),
                     reads=[("od", qi)], writes=["junk3", ("ms", qi)])
            P.op("act", lambda e: e.activation(stat[:, 56:58], stat[:, 60:62], AF.Ln, bias=EPS),
                 reads=[("ms", qi) for qi in range(2)], writes=["msln"])
            P.op("act", lambda e: e.activation(stat[:, 56:58], stat[:, 56:58], AF.Exp, scale=-0.5),
                 reads=["msln"], writes=["rs2"])
            for qi in range(2):
                tc = 2 * qb + qi
                P.op("dve", (lambda qi=qi, tc=tc: lambda e: e.scalar_tensor_tensor(ytok[:, tc, :], od[:, qi, :], stat[:, 56 + qi:57 + qi], gsub[:, l, :], ALU.mult, ALU.mult))(),
                     reads=[("od", qi), "rs2", ("gsub", l)], writes=[("ytok", tc)])

        inject = list(inject)
        pro = NS - 1
        n = len(steps)
        for i in range(pro):
            QK(i)
            BANDEXP(i)
        for i in range(n):
            if i + pro < n:
                QK(i + pro)
                BANDEXP(i + pro)
            PV(i)
            if steps[i][1] == 3 and steps[i][0] > 0:
                END_A(steps[i][0] - 1)
            if steps[i][1] == 6 and steps[i][0] > 0:
                END_B(steps[i][0] - 1)
            if inject and steps[i][1] in (9, 13):
                inject.pop(0)(7 if steps[i][0] % 2 == 0 else 5)
        END_A(7)
        END_B(7)
        while inject:
            inject.pop(0)(None)
        y_transposes(u)

    def phase_c(l, preload_only=False):
        wv_ = w_in_d[l].rearrange("(dc p) e -> p dc e", p=128)
        wna = w_na_o_d[l].rearrange("(ec p) d -> p ec d", p=128)
        wdo = w_diff_o_d[l].rearrange("(ec p) d -> p ec d", p=128)

        def load_c(c):
            s = c % 2
            P.op("pool", dma(wslotC[s][:, 0:8, :], wv_[:, :, 3072 + c * 128:3072 + (c + 1) * 128]), writes=[("w", s, 0)], dma=("w", s, 0))
            P.op("pool", dma(wslotC[s][:, 8:16, :], wv_[:, :, 4096 + c * 128:4096 + (c + 1) * 128]), writes=[("w", s, 1)], dma=("w", s, 1))
            P.op("pool", dma(wslotC[s][:, 16:20, :], wna[:, :, c * 128:(c + 1) * 128]), writes=[("w", s, 2)], dma=("w", s, 2))
            P.op("pool", dma(wslotC[s][:, 20:24, :], wdo[:, :, c * 128:(c + 1) * 128]), writes=[("w", s, 3)], dma=("w", s, 3))

        if preload_only:
            load_c(0)
            load_c(1)
            return
        it = 0
        for c in range(8):
            s = c % 2
            W = wslotC[s]
            for tb in range(4):
                bs = (it % 2) * 4
                ts = it % 2
                it += 1
                tsl = slice(tb * 512, (tb + 1) * 512)
                P.op("pe", mmk(bank(bs + 0), [(W[:, dc, :], hT[:, dc, tsl]) for dc in range(8)]),
                     reads=[("w", s, 0)] + hT_keys(tb), writes=[pk(bs + 0)], hoist=True)
                P.op("pe", mmk(bank(bs + 1), [(W[:, 8 + dc, :], hT[:, dc, tsl]) for dc in range(8)]),
                     reads=[("w", s, 1)] + hT_keys(tb), writes=[pk(bs + 1)], hoist=True)
                P.op("pe", mmk(bank(bs + 2), [(W[:, 16 + ec, :], yT[:, ec, tsl]) for ec in range(4)]),
                     reads=[("w", s, 2)] + [("yT", ec) for ec in range(4)] + [("yTb", ec) for ec in range(4)], writes=[pk(bs + 2)], hoist=True)
                P.op("pe", mmk(bank(bs + 3), [(W[:, 20 + ec, :], yT[:, 4 + ec, tsl]) for ec in range(4)]),
                     reads=[("w", s, 3)] + [("yT", 4 + ec) for ec in range(4)] + [("yTb", 4 + ec) for ec in range(4)], writes=[pk(bs + 3)], hoist=True)
                P.op("act", (lambda ts=ts, bs=bs: lambda e: e.activation(sa[ts], bank(bs + 0), AF.Sigmoid))(),
                     reads=[pk(bs + 0)], writes=[("sa", ts)])
                P.op("act", (lambda ts=ts, bs=bs: lambda e: e.activation(sd[ts], bank(bs + 1), AF.Sigmoid))(),
                     reads=[pk(bs + 1)], writes=[("sd", ts)])
                P.op("dve", (lambda ts=ts, bs=bs: lambda e: e.tensor_tensor(sa[ts], sa[ts], bank(bs + 2), ALU.mult))(),
                     reads=[pk(bs + 2), ("sa", ts)], writes=[("sa", ts)])
                P.op("dve", (lambda ts=ts, bs=bs: lambda e: e.tensor_tensor(sd[ts], sd[ts], bank(bs + 3), ALU.mult))(),
                     reads=[pk(bs + 3), ("sd", ts)], writes=[("sd", ts)])
                P.op("pool", (lambda ts=ts, c=c, tsl=tsl: lambda e: e.tensor_tensor(merged[:, c, tsl], sa[ts], sd[ts], ALU.add))(),
                     reads=[("sa", ts), ("sd", ts)], writes=[("mg", c, tb)])
            if c + 2 < 8:
                load_c(c + 2)
        wov = w_out_d[l].rearrange("(c p) e -> p c e", p=128)
        for hf in range(2):
            P.op("pool", dma(wo[:, :, hf * 512:(hf + 1) * 512], wov[:, :, hf * 512:(hf + 1) * 512]),
                 reads=([("wo", 0)] if hf == 1 else []),
                 writes=[("yT", 4), ("yT", 5), ("yT", 6), ("yT", 7), ("yTb", 4), ("yTb", 5), ("yTb", 6), ("yTb", 7), ("w1g", 1), ("w2g", 1), ("wo", hf)], dma=("wo", hf))

    def wout_phase(l, hook=None):
        it = 0
        order = [(tc, hf) for blk in range(2) for hf in range(2) for tc in range(blk * 8, blk * 8 + 8)]
        for (tc, hf) in order:
            if hook is not None and tc == 8 and hf == 0:
                hook()
            b = it % 8
            it += 1
            P.op("pe", mmk(bank(b), [(merged[:, c, tc * 128:(tc + 1) * 128], wo[:, c, hf * 512:(hf + 1) * 512]) for c in range(8)]),
                 reads=[("mg", c, tc // 4) for c in range(8)] + [("wo", hf)], writes=[pk(b)], hoist=True)
            P.op("dve", (lambda tc=tc, hf=hf, b=b: lambda e: e.tensor_tensor(xs[:, tc, hf * 512:(hf + 1) * 512], xs[:, tc, hf * 512:(hf + 1) * 512], bank(b), ALU.add))(),
                 reads=[pk(b), ("x", tc)], writes=[("x", tc)])

    def load_ffn_w(l, g):
        s = g % 2
        w1v = w_ff1_d[l].rearrange("(dc p) f -> p dc f", p=128)
        w2v = w_ff2_d[l].rearrange("(fc p) d -> p fc d", p=128)
        P.op("pool", dma(w1g[s], w1v[:, :, g * 512:(g + 1) * 512]),
             writes=[("w1g", s), ("yT", 4 * s), ("yT", 4 * s + 1), ("yTb", 4 * s), ("yTb", 4 * s + 1), ("wo", 0), ("wo", 1)], dma=("w1g", s))
        P.op("pool", dma(w2g[s], w2v[:, g * 4:(g + 1) * 4, :]),
             writes=[("w2g", s), ("yT", 4 * s + 2), ("yT", 4 * s + 3), ("yTb", 4 * s + 2), ("yTb", 4 * s + 3), ("wo", 0), ("wo", 1)], dma=("w2g", s))

    def ffn_phase(l, hook=None):
        steps = [(g, tb) for g in range(8) for tb in range(4)]
        f1 = [0]
        f2 = [0]

        def FF1(i):
            g, tb = steps[i]
            s = g % 2
            for fc in range(4):
                b = f1[0] % 4
                r = f1[0] % 2
                f1[0] += 1
                P.op("pe", mmk(bank(b), [(w1g[s][:, dc, fc * 128:(fc + 1) * 128], hT[:, dc, tb * 512:(tb + 1) * 512]) for dc in range(8)]),
                     reads=[("w1g", s)] + hT_keys(tb), writes=[pk(b)], hoist=True)
                P.op("act", (lambda r=r, b=b: lambda e: e.activation(rl[r], bank(b), AF.Relu))(),
                     reads=[pk(b)], writes=[("rl", r)])
                P.op("pool", (lambda r=r, s=s, fc=fc, tb=tb: lambda e: e.tensor_tensor(uT[s][:, fc, tb * 512:(tb + 1) * 512], rl[r], rl[r], ALU.mult))(),
                     reads=[("rl", r)], writes=[("uT", s, tb, fc)])

        def FF2(i):
            g, tb = steps[i]
            s = g % 2
            for j in range(4):
                tc = tb * 4 + j
                for hf in range(2):
                    b = 4 + f2[0] % 4
                    f2[0] += 1
                    P.op("pe", mmk(bank(b), [(uT[s][:, fc, tc * 128:(tc + 1) * 128], w2g[s][:, fc, hf * 512:(hf + 1) * 512]) for fc in range(4)]),
                         reads=[("w2g", s)] + [("uT", s, tb, fc) for fc in range(4)], writes=[pk(b)], hoist=True)
                    P.op("dve", (lambda tc=tc, hf=hf, b=b: lambda e: e.tensor_tensor(xs[:, tc, hf * 512:(hf + 1) * 512], xs[:, tc, hf * 512:(hf + 1) * 512], bank(b), ALU.add))(),
                         reads=[pk(b), ("x", tc)], writes=[("x", tc)])

        FF1(0)
        for i in range(len(steps)):
            if i + 1 < len(steps):
                FF1(i + 1)
            FF2(i)
            g, tb = steps[i]
            if tb == 3 and g + 2 < 8:
                load_ffn_w(l, g + 2)
            if hook is not None and g == 7:
                hook[0](tb)
                if tb >= 1:
                    hook[1](tb - 1)
        if hook is not None:
            hook[1](3)

    def dump_T(buf3):
        ov = out_d.rearrange("(c p h) d -> p c (h d)", c=8, p=128, h=2)
        for c in range(8):
            P.op("pool", dma(ov[:, c, :], buf3[:, c, :]), reads=[("hT", i) for i in range(16)] + [("yT", i) for i in range(8)] + [("yTb", i) for i in range(8)], dma="out")

    def dump_x():
        for g in range(4):
            P.op("sp", dma(out_d.rearrange("(t p) d -> p t d", p=128)[:, g * 4:(g + 1) * 4, :], xs[:, g * 4:(g + 1) * 4, :]),
                 reads=[("x", g * 4 + i) for i in range(4)], dma="out")

    def main():
        for li, l in enumerate(layers):
            if li == 0:
                load_unit_w(l, 0)
                load_unit_w(l, 1, after_x=True)
                for w in ("Et", "I", "Eb"):
                    load_na_bias(l, 0, w, after_x=True)
                norm_phase(n1g_d[l], False)
            if dbg == "hT":
                return dump_T(hT)
            P.op("sp", dma(dband, t5band_d), writes=[("dband", h) for h in range(4)], dma="dband")
            for h in range(4):
                P.op("dve", (lambda h=h: lambda e: e.tensor_scalar_sub(dband[:, h, :], dband[:, h, :], cst[:, h:h + 1]))(),
                     reads=[("dband", h), "cst"], writes=[("dband", h)])

            def next_bias(h, l=l):
                if h + 1 < 8:
                    return (l, h + 1)
                return None

            for u in range(8):
                if u <= 4:
                    projection(l, u)
                if u == 4:
                    P.op("dve", lambda e: e.memset(qTxb[0:64, :, 256:512], 0.0), writes=["qTx0b", "bI", "bEt", "bEb"])
                    P.op("dve", lambda e: e.memset(qTxb[64:128, :, 0:256], 0.0), writes=["qTx0b", "bI", "bEt", "bEb"])
                if u < 4:
                    na_attention(l, u, next_bias)
                else:
                    nxt = proj_chunks(l, u + 1, dset=(u + 1 - 4) % 2) if u + 1 < 8 else ()
                    diff_attention(l, u - 4, nxt)
                if u + 2 < 8:
                    load_unit_w(l, u + 2)
            if dbg == "yT":
                return dump_T(yT)
            phase_c(l, preload_only=True)
            P.barrier()
            phase_c(l)
            if dbg == "x1":
                wout_phase(l)
                return dump_x()
            n2_stats, n2_apply = norm_phase(n2g_d[l], False, hooked=True)
            wout_phase(l, lambda: (n2_stats(0), n2_stats(1)))
            load_ffn_w(l, 0)
            load_ffn_w(l, 1)
            n2_stats(2)
            n2_apply(0)
            n2_stats(3)
            n2_apply(1)
            n2_apply(2)
            n2_apply(3)
            if li + 1 < len(layers):
                ln_ = layers[li + 1]
                load_unit_w(ln_, 0)
                load_unit_w(ln_, 1)
            P.barrier()
            if li + 1 < len(layers):
                for w in ("Et", "I", "Eb"):
                    load_na_bias(layers[li + 1], 0, w)
            if li + 1 < len(layers):
                ffn_phase(l, norm_phase(n1g_d[layers[li + 1]], False, hooked=True))
                P.barrier()
            elif final:
                ffn_phase(l, norm_phase(fg_d, True, hooked=True))
            else:
                ffn_phase(l)
                dump_x()

    main()
    P.emit(final_waits=["out"])
    return nc


_CACHE = {}


def _get_nc(layers, final):
    key = (tuple(layers), final)
    if key not in _CACHE:
        _CACHE[key] = build(list(layers), final)
    return _CACHE[key]


FUSED = True


def kernel(x, t5_bias, final_norm_g, norm1_g, w_in, na_rpb, diff_lambda, diff_subln_g,
           w_na_o, w_diff_o, w_out, norm2_g, w_ff1, w_ff2):
    f = lambda a: np.ascontiguousarray(np.asarray(a, dtype=np.float32))
    x = f(x)
    t5band, t5c, nab = _host_tables(f(t5_bias), f(na_rpb))
    common = {
        "w_in": f(w_in), "w_na_o": f(w_na_o), "w_diff_o": f(w_diff_o), "w_out": f(w_out),
        "w_ff1": f(w_ff1), "w_ff2": f(w_ff2), "norm1_g": f(norm1_g), "norm2_g": f(norm2_g),
        "final_norm_g": f(final_norm_g), "diff_lambda": f(diff_lambda).reshape(512),
        "diff_subln_g": f(diff_subln_g).reshape(256), "t5c": t5c, "t5band": t5band, "nab": nab,
        "ident": np.eye(128, dtype=np.float32),
    }
    n = 8
    cur = [x[b] for b in range(n)]
    plan = [((0, 1), True)] if FUSED else [((0,), False), ((1,), True)]
    for layers, final in plan:
        nc = _get_nc(layers, final)
        in_maps = [dict(common, x=np.ascontiguousarray(cur[b])) for b in range(n)]
        res = run_bass_kernel_spmd(nc, in_maps, core_ids=list(range(n)))
        cur = [np.asarray(res.results[b]["out"], dtype=np.float32) for b in range(n)]
    return np.stack(cur, axis=0)
```

```python
import math
import contextlib
import numpy as np
import concourse.bass as bass
import concourse.mybir as mybir
from concourse.bass_utils import run_bass_kernel_spmd

F32 = mybir.dt.float32
BF16 = mybir.dt.bfloat16
AF = mybir.ActivationFunctionType
ALU = mybir.AluOpType
AX = mybir.AxisListType

ENGS = ("pe", "act", "dve", "pool", "sp")
NEG = -30000.0
PAIR_EXP = False
HOIST_PE_WAITS = True
EPS = 1e-6


class Op:
    __slots__ = ("eng", "fn", "deps", "dma", "idx", "signal", "sigval", "dmaval", "gseq", "hoist")

    def __init__(self, eng, fn, dma):
        self.gseq = 0
        self.hoist = False
        self.eng = eng
        self.fn = fn
        self.deps = set()
        self.dma = dma
        self.idx = -1
        self.signal = False
        self.sigval = 0
        self.dmaval = 0


class Prog:
    def __init__(self, nc):
        self.nc = nc
        self.ops = {e: [] for e in ENGS}
        self.lastw = {}
        self.readers = {}
        self.dma_counts = {}
        self.dma_last = {}
        self.pending = {e: None for e in ENGS}
        self.gcount = 0

    def barrier(self):
        deps = set()
        for e in ENGS:
            for o in reversed(self.ops[e]):
                if o.dma is None:
                    deps.add(o)
                    break
        for k, o in self.dma_last.items():
            deps.add(o)
        for e in ENGS:
            self.pending[e] = set(deps) | (self.pending[e] or set())

    def op(self, eng, fn, reads=(), writes=(), dma=None, hoist=False):
        o = Op(eng, fn, dma)
        o.hoist = hoist
        self.gcount += 1
        o.gseq = self.gcount
        o.idx = len(self.ops[eng])
        raw = set()
        deps = set()
        for r in reads:
            w = self.lastw.get(r)
            if w is not None:
                deps.add(w)
                raw.add(w)
        for k in writes:
            w = self.lastw.get(k)
            if w is not None:
                deps.add(w)
            for rd in self.readers.get(k, ()):
                deps.add(rd)
        keep = set()
        for d in deps:
            if d is o:
                continue
            if d.eng == eng and d.dma is None:
                if eng == "pe":
                    continue
                if d in raw and (o.idx - d.idx) <= 3:
                    keep.add(d)
                continue
            keep.add(d)
        if self.pending[eng]:
            for d in self.pending[eng]:
                if d.eng == eng and d.dma is None:
                    continue
                keep.add(d)
            self.pending[eng] = None
        o.deps = keep
        for d in keep:
            if d.dma is None:
                d.signal = True
        for r in reads:
            self.readers.setdefault(r, []).append(o)
        for k in writes:
            self.lastw[k] = o
            self.readers[k] = []
        if dma is not None:
            c = self.dma_counts.get(dma, 0) + 16
            self.dma_counts[dma] = c
            o.dmaval = c
            self.dma_last[dma] = o
        self.ops[eng].append(o)
        return o

    def emit(self, final_waits=()):
        nc = self.nc
        with contextlib.ExitStack() as st:
            esem = {e: st.enter_context(nc.semaphore("s_" + e)) for e in ENGS}
            dsem = {}
            for i, k in enumerate(self.dma_counts):
                dsem[k] = st.enter_context(nc.semaphore("d%d" % i))
            for e in ENGS:
                c = 0
                for o in self.ops[e]:
                    if o.signal:
                        c += 1
                        o.sigval = c
            block = st.enter_context(nc.Block())
            ops = self.ops
            dma_counts = self.dma_counts

            def run(eng, e):
                waited = {}
                needs = []
                for o in ops[e]:
                    need = {}
                    for d in o.deps:
                        if d.dma is not None:
                            key = ("d", d.dma)
                            val = d.dmaval
                            sem = dsem[d.dma]
                        else:
                            key = ("e", d.eng)
                            val = d.sigval
                            sem = esem[d.eng]
                        if val > need.get(key, (None, 0))[1]:
                            need[key] = (sem, val)
                    needs.append(need)
                if e == "pe" and HOIST_PE_WAITS:
                    lst = ops[e]
                    for j in range(1, len(lst)):
                        y = lst[j - 1]
                        if lst[j].hoist and needs[j] and all(d.gseq < y.gseq for d in lst[j].deps):
                            for key, (sem, val) in needs[j].items():
                                if val > needs[j - 1].get(key, (None, 0))[1]:
                                    needs[j - 1][key] = (sem, val)
                            needs[j] = {}
                for o, need in zip(ops[e], needs):
                    for key, (sem, val) in need.items():
                        if waited.get(key, 0) >= val:
                            continue
                        waited[key] = val
                        eng.wait_ge(sem, val)
                    inst = o.fn(eng)
                    if o.dma is not None:
                        inst.then_inc(dsem[o.dma], 16)
                    elif o.signal:
                        inst.then_inc(esem[e], 1)
                if e == "sp":
                    for k in final_waits:
                        eng.wait_ge(dsem[k], dma_counts[k])

            @block.tensor
            def _(eng):
                run(eng, "pe")

            @block.scalar
            def _(eng):
                run(eng, "act")

            @block.vector
            def _(eng):
                run(eng, "dve")

            @block.gpsimd
            def _(eng):
                run(eng, "pool")

            @block.sync
            def _(eng):
                run(eng, "sp")


def view(base, off_bytes, shape, dtype):
    n = int(np.prod(shape))
    esz = 2 if dtype == BF16 else 4
    assert off_bytes % 4 == 0
    a = base[:, off_bytes // 2: off_bytes // 2 + n * esz // 2]
    if dtype == F32:
        a = a.bitcast(F32)
    if len(shape) == 2:
        a = a.rearrange("p (a b) -> p a b", b=shape[1])
    elif len(shape) == 3:
        a = a.rearrange("p (a b c) -> p a b c", b=shape[1], c=shape[2])
    return a


def bcast_mid(ap, count):
    a = [list(t) for t in ap.ap]
    return bass.AP(ap.tensor, ap.offset, [a[0], [0, count]] + a[1:])


def bcast_last(ap, count):
    a = [list(t) for t in ap.ap]
    return bass.AP(ap.tensor, ap.offset, a + [[0, count]])


def mm1(out, lhsT, rhs, start=True, stop=True):
    return lambda e: e.matmul(out, lhsT, rhs, start=start, stop=stop)


def mmk(out, pairs):
    def f(e):
        n = len(pairs)
        ins = None
        for i, (a, b) in enumerate(pairs):
            ins = e.matmul(out, a, b, start=(i == 0), stop=(i == n - 1))
        return ins
    return f


def mmlist(items):
    def f(e):
        ins = None
        for (o, a, b, s0, s1) in items:
            ins = e.matmul(o, a, b, start=s0, stop=s1)
        return ins
    return f


def trlist(items, ident):
    def f(e):
        ins = None
        for (o, a) in items:
            ins = e.transpose(o, a, ident)
        return ins
    return f


def dma(out, in_):
    return lambda e: e.dma_start(out=out, in_=in_)


def _t5_bucket_np(rel, mode):
    rel = rel.astype(np.int32)
    ret = np.where(rel > 0, 16, 0)
    n = np.abs(rel)
    nf = np.maximum(n, 1).astype(np.float32)
    v = np.log(nf / np.float32(8)) / np.float32(math.log(16)) * np.float32(8)
    if mode == "round":
        large = 8 + np.round(v).astype(np.int32)
    else:
        large = 8 + v.astype(np.int32)
    large = np.minimum(large, 15)
    return ret + np.where(n < 8, n, large)


T5_MODE = "trunc"


def _na_tile_list():
    tiles = [(5, 5 + d) for d in (-2, -1, 0, 1, 2)]
    tiles += [(0, m) for m in range(4)] + [(1, m) for m in range(4)]
    tiles += [(14, m) for m in range(12, 16)] + [(15, m) for m in range(12, 16)]
    return tiles


def _host_tables(t5_bias, na_rpb):
    kl = np.arange(128)[:, None]
    j = np.arange(384)[None, :]
    bucket = _t5_bucket_np(kl + 128 - j, T5_MODE)
    t5band = np.ascontiguousarray(np.transpose(t5_bias[bucket], (0, 2, 1))).astype(np.float32)
    t5c = np.ascontiguousarray(t5_bias[[15, 31], :]).astype(np.float32).reshape(8)
    tiles = _na_tile_list()
    k_l = np.arange(128)[:, None]
    q_l = np.arange(128)[None, :]
    a, kc = k_l // 64, k_l % 64
    b, qc = q_l // 64, q_l % 64
    dr_idx = np.zeros((21, 128, 128), np.int64)
    dc_idx = np.zeros((21, 128, 128), np.int64)
    valid = np.zeros((21, 128, 128), bool)
    for t, (tc, m) in enumerate(tiles):
        r = 2 * tc + b
        krow = 2 * m + a
        r0 = np.clip(r - 4, 0, 24)
        vrow = (krow >= r0) & (krow <= r0 + 7)
        c0 = np.clip(qc - 8, 0, 48)
        vcol = (kc >= c0) & (kc < c0 + 16)
        dr = krow - r
        dc = np.clip(kc - qc, -15, 15)
        v = vrow & vcol
        valid[t] = v
        dr_idx[t] = np.where(v, dr + 7, 0)
        dc_idx[t] = np.where(v, dc + 15, 0)
    g = na_rpb[:, :, dr_idx, dc_idx]
    g = np.where(valid[None, None], g, np.float32(NEG)).astype(np.float32)
    nab = np.ascontiguousarray(np.transpose(g, (0, 1, 3, 2, 4)))
    return t5band, t5c, nab


def build(layers, final, dbg=None):
    nc = bass.Bass("TRN2", target_bir_lowering=False)
    dt_in = lambda name, shape: nc.dram_tensor(name, shape, F32, kind="ExternalInput").ap()
    x_d = dt_in("x", [2048, 1024])
    w_in_d = dt_in("w_in", [2, 1024, 5120])
    w_na_o_d = dt_in("w_na_o", [2, 512, 1024])
    w_diff_o_d = dt_in("w_diff_o", [2, 512, 1024])
    w_out_d = dt_in("w_out", [2, 1024, 1024])
    w_ff1_d = dt_in("w_ff1", [2, 1024, 4096])
    w_ff2_d = dt_in("w_ff2", [2, 4096, 1024])
    n1g_d = dt_in("norm1_g", [2, 1024])
    n2g_d = dt_in("norm2_g", [2, 1024])
    fg_d = dt_in("final_norm_g", [1024])
    lam_d = dt_in("diff_lambda", [512])
    subg_d = dt_in("diff_subln_g", [256])
    t5c_d = dt_in("t5c", [8])
    t5band_d = dt_in("t5band", [128, 4, 384])
    nab_d = dt_in("nab", [2, 8, 128, 21, 128])
    ident_d = dt_in("ident", [128, 128])
    out_d = nc.dram_tensor("out", [2048, 1024], F32, kind="ExternalOutput").ap()

    xs = nc.alloc_sbuf_tensor("xs", [128, 16, 1024], F32)
    hT = nc.alloc_sbuf_tensor("hT", [128, 8, 2048], BF16)
    yTf = nc.alloc_sbuf_tensor("yTf", [128, 16384], BF16)
    ident = nc.alloc_sbuf_tensor("ident_sb", [128, 128], BF16)
    gbc = nc.alloc_sbuf_tensor("gbc", [128, 1024], F32)
    cst = nc.alloc_sbuf_tensor("cst", [128, 32], F32)
    gsub = nc.alloc_sbuf_tensor("gsub", [128, 2, 128], F32)
    stat = nc.alloc_sbuf_tensor("stat", [128, 64], F32)
    junk2 = nc.alloc_sbuf_tensor("junk2", [128, 128], BF16)
    junk3 = nc.alloc_sbuf_tensor("junk3", [128, 128], BF16)
    hsb = nc.alloc_sbuf_tensor("hsb", [128, 2048], BF16)
    hs = [hsb[:, 0:1024], hsb[:, 1024:2048]]
    kTb = hsb[:, :]
    DYN = 69760
    dyn = nc.alloc_sbuf_tensor("dyn", [128, DYN // 2], BF16)
    assert nc.sbuf_bytes_remaining >= 0

    yT = yTf[:, :].rearrange("p (c t) -> p c t", t=2048)
    w1g = [view(yTf, s * 16384 + 0, [8, 512], BF16) for s in range(2)]
    w2g = [view(yTf, s * 16384 + 8192, [4, 1024], BF16) for s in range(2)]
    wo = view(yTf, 16384, [8, 1024], BF16)

    wslot = [view(dyn, s * 6144, [3, 8, 128], BF16) for s in range(2)]
    wslotC = [view(dyn, s * 6144, [24, 128], BF16) for s in range(2)]
    bI = view(dyn, 12288, [5, 128], F32)
    bEt = view(dyn, 14848, [8, 128], F32)
    bEb = view(dyn, 18944, [8, 128], F32)
    A0 = 23040
    qT = view(dyn, A0, [2048], BF16)
    kT = view(dyn, A0 + 4096, [2048], BF16)
    vv = [view(dyn, A0 + 8192 + s * 4160, [16, 130], BF16) for s in range(2)]
    PTn = [view(dyn, A0 + 16512 + s * 1280, [640], BF16) for s in range(3)]
    PTd = [view(dyn, A0 + 20352 + s * 1024, [2, 256], BF16) for s in range(4)]
    PTd2 = [view(dyn, A0 + 20352 + s * 2048, [1024], BF16) for s in range(2)]
    ytok = view(dyn, A0 + 28288, [16, 128], BF16)
    dband = view(dyn, A0 + 32384, [4, 384], F32)
    od = view(dyn, A0 + 24448, [4, 128], F32)
    tq = view(dyn, A0 + 26496, [128], F32)
    qTxb = view(dyn, 12288, [8, 512], BF16)
    qTx = view(dyn, 61568, [8, 512], BF16)
    lamb = hs[0].bitcast(F32)
    subgb = hs[1][:, 0:512].bitcast(F32)
    merged = view(dyn, A0, [8, 2048], BF16)
    sa = [view(dyn, 12288 + s * 4096, [512], F32) for s in range(2)]
    sd = [view(dyn, 12288 + 2048 + s * 4096, [512], F32) for s in range(2)]
    uT = [view(dyn, A0 + s * 16384, [4, 2048], BF16) for s in range(2)]
    rl = [view(dyn, A0 + 32768 + s * 2048, [512], F32) for s in range(2)]
    of32 = [view(dyn, s * 4096, [1024], F32) for s in range(2)]

    psp = [nc.alloc_psum_tensor("psp%d" % i, [128, 2, 512], F32) for i in range(4)]

    def bank(i):
        return psp[i // 2][:, i % 2, :]

    def bank_bf(i):
        return psp[i // 2].bitcast(BF16)[:, i % 2, :]

    def pflat(s, n):
        return bass.AP(psp[s], 0, [[1024, 128], [1, n]])

    P = Prog(nc)
    pk = lambda i: ("p", i)

    P.op("sp", dma(cst[:, 0:8], t5c_d.partition_broadcast(128)), writes=["cst"], dma="cst")
    P.op("sp", dma(lamb, lam_d.partition_broadcast(128)), writes=["lamb"], dma="lamb")
    P.op("sp", dma(subgb, subg_d.partition_broadcast(128)), writes=["subgb"], dma="subgb")
    xv = x_d.rearrange("(t p) d -> p t d", p=128)
    for g in range(8):
        P.op("sp", dma(xs[:, g * 2:(g + 1) * 2, :], xv[:, g * 2:(g + 1) * 2, :]),
             writes=[("x", g * 2 + i) for i in range(2)], dma=("x", g))
    P.op("pool", dma(ident[:], ident_d), writes=["ident"], dma="ident")
    P.op("dve", lambda e: e.memset(qTx[0:64, :, 256:512], 0.0), writes=["qTx0"])
    P.op("dve", lambda e: e.memset(qTx[64:128, :, 0:256], 0.0), writes=["qTx0"])
    lam_init = [0.8 - 0.6 * math.exp(-0.3 * l) for l in range(2)]
    for l in range(2):
        for j in range(2):
            a0 = l * 256 + j * 128
            P.op("dve", (lambda a0=a0: lambda e: e.tensor_tensor(gbc[:, 0:64], lamb[:, a0:a0 + 64], lamb[:, a0 + 64:a0 + 128], ALU.mult))(),
                 reads=["lamb"], writes=["gbc"])
            col = 8 + l * 4 + j
            P.op("dve", (lambda col=col: lambda e: e.reduce_sum(cst[:, col:col + 1], gbc[:, 0:64], axis=AX.X))(),
                 reads=["gbc"], writes=[("cc", col)])
            P.op("act", (lambda col=col: lambda e: e.activation(cst[:, col:col + 1], cst[:, col:col + 1], AF.Exp))(),
                 reads=[("cc", col)], writes=[("cc", col)])
        c0 = 8 + l * 4
        P.op("dve", (lambda c0=c0: lambda e: e.tensor_tensor(cst[:, c0 + 2:c0 + 3], cst[:, c0:c0 + 1], cst[:, c0 + 1:c0 + 2], ALU.subtract))(),
             reads=[("cc", c0), ("cc", c0 + 1)], writes=[("cc", c0 + 2)])
        P.op("dve", (lambda c0=c0, li=lam_init[l]: lambda e: e.tensor_scalar(cst[:, c0 + 3:c0 + 4], cst[:, c0 + 2:c0 + 3], li, -1.0, ALU.add, ALU.mult))(),
             reads=[("cc", c0 + 2)], writes=[("neglam", l)])
        P.op("dve", (lambda l=l: lambda e: e.tensor_scalar_mul(gsub[:, l, :], subgb[:, l * 128:(l + 1) * 128], 1.0 - lam_init[l]))(),
             reads=["subgb"], writes=[("gsub", l)])
    neglam = [cst[:, 8 + l * 4 + 3: 8 + l * 4 + 4] for l in range(2)]

    trn = [0]

    def norm_phase(g_row, is_final, hooked=False):
        P.op("sp", dma(gbc[:], g_row.partition_broadcast(128)), writes=["gbc"], dma="gbc")
        sq_out = bass.AP(junk2, 0, [[128, 128], [0, 8], [1, 128]])

        def stats(grp):
            for j in range(4):
                tc = grp * 4 + j
                P.op("act", (lambda tc=tc: lambda e: e.activation(sq_out, xs[:, tc, :].rearrange("p (a b) -> p a b", a=8), AF.Square,
                                                                   scale=1.0 / 32, accum_out=stat[:, tc:tc + 1]))(),
                     reads=[("x", tc)], writes=["junk2", ("ss", tc)])
            P.op("act", (lambda grp=grp: lambda e: e.activation(stat[:, 16 + grp * 4:20 + grp * 4], stat[:, grp * 4:grp * 4 + 4], AF.Ln, bias=EPS))(),
                 reads=[("ss", grp * 4 + i) for i in range(4)], writes=[("ln", grp)])
            P.op("act", (lambda grp=grp: lambda e: e.activation(stat[:, 32 + grp * 4:36 + grp * 4], stat[:, 16 + grp * 4:20 + grp * 4], AF.Exp, scale=-0.5))(),
                 reads=[("ln", grp)], writes=[("rstd", grp)])

        def apply(grp):
            for j in range(4):
                tc = grp * 4 + j
                s = tc % 2
                if is_final:
                    P.op("dve", (lambda tc=tc, s=s: lambda e: e.scalar_tensor_tensor(of32[s], xs[:, tc, :], stat[:, 32 + tc:33 + tc], gbc[:], ALU.mult, ALU.mult))(),
                         reads=[("x", tc), ("rstd", grp), "gbc"], writes=[("of", s)] + [("w", ws_, bi_) for ws_ in range(2) for bi_ in range(4)])
                    P.op("sp", dma(out_d[tc * 128:(tc + 1) * 128, :], of32[s]), reads=[("of", s)], dma="out")
                    continue
                P.op("dve", (lambda tc=tc, s=s: lambda e: e.scalar_tensor_tensor(hs[s], xs[:, tc, :], stat[:, 32 + tc:33 + tc], gbc[:], ALU.mult, ALU.mult))(),
                     reads=[("x", tc), ("rstd", grp), "gbc"], writes=[("hs", s)])
                b = 6 + (trn[0] % 2)
                trn[0] += 1
                pb = bank_bf(b)
                P.op("pe", trlist([(pb[:, c * 128:(c + 1) * 128], hs[s][:, c * 128:(c + 1) * 128]) for c in range(8)], ident[:]),
                     reads=[("hs", s), "ident"], writes=[pk(b)])
                src = pb[:, 0:1024].rearrange("p (c t) -> p c t", t=128)
                dst = hT[:, :, tc * 128:(tc + 1) * 128]
                if tc % 2 == 0:
                    P.op("act", (lambda dst=dst, src=src: lambda e: e.activation(dst, src, AF.Copy))(),
                         reads=[pk(b)], writes=[("hT", tc)])
                else:
                    P.op("dve", (lambda dst=dst, src=src: lambda e: e.tensor_copy(dst, src))(),
                         reads=[pk(b)], writes=[("hT", tc)])

        if hooked:
            return stats, apply
        stats(0)
        for grp in range(4):
            if grp + 1 < 4:
                stats(grp + 1)
            apply(grp)

    def hT_keys(tb):
        return [("hT", tb * 4 + i) for i in range(4)]

    def unit_cols(u):
        if u < 4:
            return (u * 128, 512 + u * 128, 1024 + u * 128)
        h = u - 4
        return (1536 + h * 128, 2048 + h * 128, 2560 + h * 128)

    def load_unit_w(l, u, after_x=False):
        s = u % 2
        wv_ = w_in_d[l].rearrange("(dc p) e -> p dc e", p=128)
        rd = [("x", 15)] if after_x else []
        for bi, c0 in enumerate(unit_cols(u)):
            P.op("pool", dma(wslot[s][:, bi], wv_[:, :, c0:c0 + 128]), reads=rd, writes=[("w", s, bi)], dma=("w", s, bi))

    def load_na_bias(l, h, which, after_x=False):
        src = nab_d[l, h]
        rd = [("x", 15)] if after_x else []
        if which == "I":
            P.op("sp", dma(bI, src[:, 0:5, :]), reads=rd, writes=["bI"], dma="bI")
        elif which == "Et":
            P.op("sp", dma(bEt, src[:, 5:13, :]), reads=rd, writes=["bEt"], dma="bEt")
        else:
            P.op("sp", dma(bEb, src[:, 13:21, :]), reads=rd, writes=["bEb"], dma="bEb")

    pjn = [0]

    def proj_chunks(l, u, banks=None, dset=0):
        s = u % 2
        vs = u % 2
        is_na = u < 4
        wq, wk, wv_ = wslot[s][:, 0], wslot[s][:, 1], wslot[s][:, 2]
        kT_s = kT if dset == 0 else kTb
        qTx_s = qTx if dset == 0 else qTxb
        kkey = "kT" if dset == 0 else "kTb"
        hsk = [] if dset == 0 else [("hs", 0), ("hs", 1)]
        z0 = "qTx0" if dset == 0 else "qTx0b"
        chunks = []

        cur = [None]

        def nb():
            if cur[0] is not None:
                return cur[0]
            b = pjn[0] % 6
            pjn[0] += 1
            return b

        def qk_chunk(which, wt, tb):
            b = nb()
            P.op("pe", mmk(bank(b), [(wt[:, dc, :], hT[:, dc, tb * 512:(tb + 1) * 512]) for dc in range(8)]),
                 reads=[("w", s, which)] + hT_keys(tb), writes=[pk(b)])
            if (not is_na) and which == 0:
                d0 = qTx_s[0:64, 2 * tb:2 * tb + 2, 0:256]
                d1 = qTx_s[64:128, 2 * tb:2 * tb + 2, 256:512]
                s0 = bank(b)[0:64, :].rearrange("p (a c) -> p a c", a=2)
                s1 = bank(b)[64:128, :].rearrange("p (a c) -> p a c", a=2)
                P.op("dve", (lambda d0=d0, s0=s0: lambda e: e.tensor_scalar_mul(d0, s0, 0.125))(),
                     reads=[pk(b), z0], writes=[("qTa", dset, tb)])
                P.op("dve", (lambda d1=d1, s1=s1: lambda e: e.tensor_scalar_mul(d1, s1, 0.125))(),
                     reads=[pk(b), z0], writes=[("qTb", dset, tb)])
                return
            if which == 0:
                d0 = qTx[0:64, 2 * tb:2 * tb + 2, 0:256]
                d1 = qTx[64:128, 2 * tb:2 * tb + 2, 256:512]
                s0 = bank(b)[0:64, :].rearrange("p (a c) -> p a c", a=2)
                s1 = bank(b)[64:128, :].rearrange("p (a c) -> p a c", a=2)
                P.op("act", (lambda d0=d0, s0=s0: lambda e: e.activation(d0, s0, AF.Copy, scale=0.125))(),
                     reads=[pk(b), "qTx0"], writes=[("qTa", 0, tb)])
                P.op("act", (lambda d1=d1, s1=s1: lambda e: e.activation(d1, s1, AF.Copy, scale=0.125))(),
                     reads=[pk(b), "qTx0"], writes=[("qTb", 0, tb)])
            elif is_na:
                dst = kT[:, tb * 512:(tb + 1) * 512]
                P.op("act", (lambda dst=dst, b=b: lambda e: e.activation(dst, bank(b), AF.Copy))(),
                     reads=[pk(b)], writes=[("kT", tb)])
            else:
                dst = kT_s[:, tb * 512:(tb + 1) * 512]
                P.op("dve", (lambda dst=dst, b=b: lambda e: e.tensor_copy(dst, bank(b)))(),
                     reads=[pk(b)], writes=[(kkey, tb)] + hsk)

        def v_chunk(tg):
            if tg == 0:
                if is_na:
                    ones_ap = vv[vs][:, :, :].rearrange("p t (h e) -> p t h e", h=2)[:, :, :, 64:65]
                else:
                    ones_ap = vv[vs][:, :, 128:129]
                P.op("pool", (lambda ones_ap=ones_ap: lambda e: e.memset(ones_ap, 1.0))(), writes=[("v1", vs)])
            b = nb()
            items = []
            for j in range(4):
                tc = tg * 4 + j
                for dc in range(8):
                    items.append((bank(b)[:, j * 128:(j + 1) * 128], hT[:, dc, tc * 128:(tc + 1) * 128], wv_[:, dc, :], dc == 0, dc == 7))
            P.op("pe", mmlist(items), reads=[("w", s, 2)] + hT_keys(tg), writes=[pk(b)])
            if is_na:
                dst = vv[vs][:, tg * 4:(tg + 1) * 4, :].rearrange("p t (h e) -> p t h e", h=2)[:, :, :, 0:64]
                src = bank(b).rearrange("p (t h e) -> p t h e", t=4, h=2)
                P.op("act", (lambda dst=dst, src=src: lambda e: e.activation(dst, src, AF.Copy))(),
                     reads=[pk(b)], writes=[("v", vs, tg)])
            else:
                dst = vv[vs][:, tg * 4:(tg + 1) * 4, 0:128]
                src = bank(b).rearrange("p (t e) -> p t e", t=4)
                P.op("dve", (lambda dst=dst, src=src: lambda e: e.tensor_copy(dst, src))(),
                     reads=[pk(b)], writes=[("v", vs, tg)])

        def wrap(f):
            def g(bk=None):
                cur[0] = bk
                f()
                cur[0] = None
            return g

        for which, wt in ((0, wq), (1, wk)):
            for tb in range(4):
                chunks.append(wrap((lambda which=which, wt=wt, tb=tb: lambda: qk_chunk(which, wt, tb))()))
        for tg in range(4):
            chunks.append(wrap((lambda tg=tg: lambda: v_chunk(tg))()))
        return chunks

    def projection(l, u):
        for c in proj_chunks(l, u):
            c()

    def y_transposes(u, b1=None):
        if b1 is None:
            b1 = 5 if u >= 4 else 6
        for half in range(2):
            b = 7 if half == 0 else b1
            pb = bank_bf(b)
            P.op("pe", trlist([(pb[:, j * 128:(j + 1) * 128], ytok[:, half * 8 + j, :]) for j in range(8)], ident[:]),
                 reads=[("ytok", half * 8 + j) for j in range(8)] + ["ident"], writes=[pk(b)])
            dst = yT[:, u, half * 1024:(half + 1) * 1024]
            if half == 0:
                P.op("act", (lambda dst=dst, pb=pb: lambda e: e.activation(dst, pb[:, 0:1024], AF.Copy))(),
                     reads=[pk(b)], writes=[("yT", u)])
            else:
                P.op("dve", (lambda dst=dst, pb=pb: lambda e: e.tensor_copy(dst, pb[:, 0:1024]))(),
                     reads=[pk(b)], writes=[("yTb", u)])

    def na_chunks(tc):
        if tc <= 1:
            return [0, 1, 2, 3]
        if tc >= 14:
            return [12, 13, 14, 15]
        return [tc - 2, tc - 1, tc, tc + 1, tc + 2]

    def na_attention(l, u, next_bias):
        vs = u % 2
        obank = [6, 7]
        steps = [(hh, tc) for hh in range(2) for tc in range(16)]
        NS = 3

        def QK(i):
            hh, tc = steps[i]
            p0 = hh * 64
            s = i % NS
            ms = na_chunks(tc)
            items = []
            qsl = qTx[:, tc // 2, hh * 256 + (tc % 2) * 128: hh * 256 + (tc % 2) * 128 + 128]
            for j, m in enumerate(ms):
                items.append((pflat(s, 640)[:, j * 128:(j + 1) * 128], kT[:, m * 128:(m + 1) * 128], qsl, True, True))
            P.op("pe", mmlist(items), reads=[("qTa", 0, tc // 4), ("qTb", 0, tc // 4)] + sorted(set(("kT", m // 4) for m in ms)),
                 writes=[pk(2 * s), pk(2 * s + 1)])

        def ADDEXP(i):
            hh, tc = steps[i]
            s = i % NS
            n = len(na_chunks(tc)) * 128
            if tc == 0:
                bt, bkey = bEt[:, 0:4, :], "bEt"
            elif tc == 1:
                bt, bkey = bEt[:, 4:8, :], "bEt"
            elif tc == 14:
                bt, bkey = bEb[:, 0:4, :], "bEb"
            elif tc == 15:
                bt, bkey = bEb[:, 4:8, :], "bEb"
            else:
                bt, bkey = bI[:, 0:5, :], "bI"
            btf = bt.rearrange("p a b -> p (a b)")
            P.op("dve", (lambda s=s, n=n, btf=btf: lambda e: e.tensor_tensor(pflat(s, n), pflat(s, n), btf, ALU.add))(),
                 reads=[pk(2 * s), pk(2 * s + 1), bkey], writes=[pk(2 * s), pk(2 * s + 1)])
            P.op("act", (lambda s=s, n=n: lambda e: e.activation(PTn[s][:, 0:n], pflat(s, n), AF.Exp))(),
                 reads=[pk(2 * s), pk(2 * s + 1)], writes=[("PTn", s)])

        def PV(i):
            hh, tc = steps[i]
            s = i % NS
            ms = na_chunks(tc)
            ob = obank[(tc // 4) % 2]
            o_ap = bank(ob)[:, (tc % 4) * 65:(tc % 4) * 65 + 65]
            items = []
            for j, m in enumerate(ms):
                items.append((o_ap, PTn[s][:, j * 128:(j + 1) * 128], vv[vs][:, m, hh * 65:hh * 65 + 65], j == 0, j == len(ms) - 1))
            P.op("pe", mmlist(items), reads=[("PTn", s), ("v1", vs)] + sorted(set(("v", vs, m // 4) for m in ms)),
                 writes=[pk(ob)])

        def NORM(hh, g):
            ob = obank[g % 2]
            o3 = bank(ob)[:, 0:260].rearrange("p (t e) -> p t e", e=65)
            rec = stat[:, 48:52]
            P.op("dve", (lambda o3=o3, rec=rec: lambda e: e.reciprocal(rec, o3[:, :, 64]))(),
                 reads=[pk(ob)], writes=["rec"])
            dst = ytok[:, g * 4:(g + 1) * 4, hh * 64:(hh + 1) * 64]
            P.op("dve", (lambda o3=o3, rec=rec, dst=dst: lambda e: e.tensor_tensor(dst, o3[:, :, 0:64], bcast_last(rec, 64), ALU.mult))(),
                 reads=[pk(ob), "rec"], writes=[("ytok", g * 4 + i_) for i_ in range(4)])

        pro = NS - 1
        n = len(steps)
        for i in range(pro):
            QK(i)
            ADDEXP(i)
        for i in range(n):
            if i + pro < n:
                QK(i + pro)
                ADDEXP(i + pro)
            PV(i)
            hh, tc = steps[i]
            if tc % 4 == 1 and i >= 2:
                ph, pt = steps[i - 2]
                NORM(ph, pt // 4)
            nb = next_bias(2 * u + hh)
            if nb is not None:
                if tc == 2:
                    load_na_bias(nb[0], nb[1], "Et")
                elif tc == 11:
                    load_na_bias(nb[0], nb[1], "I")
                elif tc == 13:
                    load_na_bias(nb[0], nb[1], "Eb")
        NORM(1, 3)

    def diff_attention(l, h, inject=(), prev_y=None):
        u = 4 + h
        dset = h % 2
        kT_s = kT if dset == 0 else kTb
        qTx_s = qTx if dset == 0 else qTxb
        kkey = "kT" if dset == 0 else "kTb"
        hsk = [] if dset == 0 else [("hs", 0), ("hs", 1)]
        vs = u % 2
        c15 = cst[:, h:h + 1]
        c31 = cst[:, 4 + h:5 + h]
        steps = [(qb, kc) for qb in range(8) for kc in range(16)]
        NS = 4

        def oblock(st_, qi, mp):
            bi = qi * 2 + mp
            bk = 4 + 2 * st_ + bi // 3
            c = (bi % 3) * 129
            return bk, bank(bk)[:, c:c + 129]

        def S3(b):
            return bank(b).rearrange("p (m c) -> p m c", m=2)

        def QK(i):
            qb, kc = steps[i]
            b = i % NS
            items = [(bank(b), kT_s[:, kc * 128:(kc + 1) * 128], qTx_s[:, qb, :], True, True)]
            P.op("pe", mmlist(items), reads=[("qTa", dset, qb // 2), ("qTb", dset, qb // 2), (kkey, kc // 4)] + hsk, writes=[pk(b)])

        def BANDEXP(i):
            qb, kc = steps[i]
            b = i % NS
            qlo = max(kc - 1, 2 * qb)
            qhi = min(kc + 1, 2 * qb + 1)
            if qlo <= qhi:
                c0 = (qlo - 2 * qb) * 128
                n = (qhi - qlo + 1) * 128
                j0 = (qlo - (kc - 1)) * 128
                P.op("dve", (lambda b=b, c0=c0, n=n, j0=j0: lambda e: e.tensor_tensor(
                    S3(b)[:, :, c0:c0 + n], S3(b)[:, :, c0:c0 + n], bcast_mid(dband[:, h, j0:j0 + n], 2), ALU.add))(),
                    reads=[pk(b), ("dband", h)], writes=[pk(b)])
            def kind(j):
                qb_, kc_ = steps[j]
                return min(max(kc_ - 1 - 2 * qb_, 0), 2)

            ncl = kind(i) * 128
            if PAIR_EXP and i % 2 == 0 and i + 1 < len(steps) and kind(i) != 1 and kind(i + 1) == kind(i):
                return
            if PAIR_EXP and i % 2 == 1 and kind(i) != 1 and kind(i - 1) == kind(i):
                cb = c15 if ncl == 0 else c31
                j = (b - 1) // 2
                P.op("act", (lambda j=j, cb=cb: lambda e: e.activation(PTd2[j], pflat(j, 1024), AF.Exp, bias=cb))(),
                     reads=[pk(b - 1), pk(b), "cst"], writes=[("PTd", b - 1), ("PTd", b)])
                return
            if ncl == 0 or ncl == 256:
                cb = c15 if ncl == 0 else c31
                P.op("act", (lambda b=b, cb=cb: lambda e: e.activation(PTd[b].rearrange("p m c -> p (m c)"), bank(b), AF.Exp, bias=cb))(),
                     reads=[pk(b), "cst"], writes=[("PTd", b)])
            else:
                P.op("act", (lambda b=b: lambda e: e.activation(PTd[b][:, :, 0:128], S3(b)[:, :, 0:128], AF.Exp, bias=c31))(),
                     reads=[pk(b), "cst"], writes=[("PTd", b)])
                P.op("act", (lambda b=b: lambda e: e.activation(PTd[b][:, :, 128:256], S3(b)[:, :, 128:256], AF.Exp, bias=c15))(),
                     reads=[pk(b), "cst", ("PTd", b)], writes=[("PTd", b)])

        def PV(i):
            qb, kc = steps[i]
            b = i % NS
            st_ = qb % 2
            items = []
            wk = set()
            started = set()
            for mp in range(2):
                for qi in range(2):
                    bk, o_ap = oblock(st_, qi, mp)
                    wk.add(pk(bk))
                    s0 = (kc == 0) and (bk not in started)
                    started.add(bk)
                    items.append((o_ap, PTd[b][:, mp, qi * 128:(qi + 1) * 128], vv[vs][:, kc, 0:129], s0, kc == 15))
            P.op("pe", mmlist(items), reads=[("PTd", b), ("v1", vs), ("v", vs, kc // 4)], writes=sorted(wk))

        def END_A(qb):
            st_ = qb % 2
            for qi in range(2):
                bk1, o1 = oblock(st_, qi, 0)
                bk2, o2 = oblock(st_, qi, 1)
                r1 = stat[:, 52 + 2 * qi:53 + 2 * qi]
                r2 = stat[:, 53 + 2 * qi:54 + 2 * qi]
                P.op("dve", (lambda o1=o1, r1=r1: lambda e: e.reciprocal(r1, o1[:, 128:129]))(), reads=[pk(bk1)], writes=[("r1", qi)])
                P.op("dve", (lambda o2=o2, r2=r2: lambda e: e.reciprocal(r2, o2[:, 128:129]))(), reads=[pk(bk2)], writes=[("r2", qi)])
                P.op("dve", (lambda o2=o2, r2=r2: lambda e: e.tensor_scalar(tq, o2[:, 0:128], r2, neglam[l], ALU.mult, ALU.mult))(),
                     reads=[pk(bk2), ("r2", qi), ("neglam", l)], writes=["tq"])
                P.op("dve", (lambda o1=o1, r1=r1, qi=qi: lambda e: e.scalar_tensor_tensor(od[:, qi, :], o1[:, 0:128], r1, tq, ALU.mult, ALU.add))(),
                     reads=[pk(bk1), ("r1", qi), "tq"], writes=[("od", qi)])

        def END_B(qb):
            for qi in range(2):
                P.op("dve", (lambda qi=qi: lambda e: e.scalar_tensor_tensor(junk3[:], od[:, qi, :], 1.0 / 128.0, od[:, qi, :], ALU.mult, ALU.mult,
                                                                            accum_out=stat[:, 60 + qi:61 + qi]))(),
                     reads=[("od", qi)], writes=["junk3", ("ms", qi)])
            P.op("act", lambda e: e.activation(stat[:, 56:58], stat[:, 60:62], AF.Ln, bias=EPS),
                 reads=[("ms", qi) for qi in range(2)], writes=["msln"])
            P.op("act", lambda e: e.activation(stat[:, 56:58], stat[:, 56:58], AF.Exp, scale=-0.5),
                 reads=["msln"], writes=["rs2"])
            for qi in range(2):
                tc = 2 * qb + qi
                P.op("dve", (lambda qi=qi, tc=tc: lambda e: e.scalar_tensor_tensor(ytok[:, tc, :], od[:, qi, :], stat[:, 56 + qi:57 + qi], gsub[:, l, :], ALU.mult, ALU.mult))(),
                     reads=[("od", qi), "rs2", ("gsub", l)], writes=[("ytok", tc)])

        inject = list(inject)
        pro = NS - 1
        n = len(steps)
        for i in range(pro):
            QK(i)
            BANDEXP(i)
        for i in range(n):
            if i + pro < n:
                QK(i + pro)
                BANDEXP(i + pro)
            PV(i)
            if steps[i][1] == 3 and steps[i][0] > 0:
                END_A(steps[i][0] - 1)
            if steps[i][1] == 6 and steps[i][0] > 0:
                END_B(steps[i][0] - 1)
            if prev_y is not None and i == 4:
                y_transposes(prev_y, 6)
            if inject and steps[i][1] in (9, 13):
                inject.pop(0)(7 if steps[i][0] % 2 == 0 else 5)
        END_A(7)
        END_B(7)
        while inject:
            inject.pop(0)(None)

    def phase_c(l, preload_only=False):
        wv_ = w_in_d[l].rearrange("(dc p) e -> p dc e", p=128)
        wna = w_na_o_d[l].rearrange("(ec p) d -> p ec d", p=128)
        wdo = w_diff_o_d[l].rearrange("(ec p) d -> p ec d", p=128)

        def load_c(c):
            s = c % 2
            P.op("pool", dma(wslotC[s][:, 0:8, :], wv_[:, :, 3072 + c * 128:3072 + (c + 1) * 128]), writes=[("w", s, 0)], dma=("w", s, 0))
            P.op("pool", dma(wslotC[s][:, 8:16, :], wv_[:, :, 4096 + c * 128:4096 + (c + 1) * 128]), writes=[("w", s, 1)], dma=("w", s, 1))
            P.op("pool", dma(wslotC[s][:, 16:20, :], wna[:, :, c * 128:(c + 1) * 128]), writes=[("w", s, 2)], dma=("w", s, 2))
            P.op("pool", dma(wslotC[s][:, 20:24, :], wdo[:, :, c * 128:(c + 1) * 128]), writes=[("w", s, 3)], dma=("w", s, 3))

        if preload_only:
            load_c(0)
            load_c(1)
            return
        it = 0
        for c in range(8):
            s = c % 2
            W = wslotC[s]
            for tb in range(4):
                bs = (it % 2) * 4
                ts = it % 2
                it += 1
                tsl = slice(tb * 512, (tb + 1) * 512)
                P.op("pe", mmk(bank(bs + 0), [(W[:, dc, :], hT[:, dc, tsl]) for dc in range(8)]),
                     reads=[("w", s, 0)] + hT_keys(tb), writes=[pk(bs + 0)], hoist=True)
                P.op("pe", mmk(bank(bs + 1), [(W[:, 8 + dc, :], hT[:, dc, tsl]) for dc in range(8)]),
                     reads=[("w", s, 1)] + hT_keys(tb), writes=[pk(bs + 1)], hoist=True)
                P.op("pe", mmk(bank(bs + 2), [(W[:, 16 + ec, :], yT[:, ec, tsl]) for ec in range(4)]),
                     reads=[("w", s, 2)] + [("yT", ec) for ec in range(4)] + [("yTb", ec) for ec in range(4)], writes=[pk(bs + 2)], hoist=True)
                P.op("pe", mmk(bank(bs + 3), [(W[:, 20 + ec, :], yT[:, 4 + ec, tsl]) for ec in range(4)]),
                     reads=[("w", s, 3)] + [("yT", 4 + ec) for ec in range(4)] + [("yTb", 4 + ec) for ec in range(4)], writes=[pk(bs + 3)], hoist=True)
                P.op("act", (lambda ts=ts, bs=bs: lambda e: e.activation(sa[ts], bank(bs + 0), AF.Sigmoid))(),
                     reads=[pk(bs + 0)], writes=[("sa", ts)])
                P.op("act", (lambda ts=ts, bs=bs: lambda e: e.activation(sd[ts], bank(bs + 1), AF.Sigmoid))(),
                     reads=[pk(bs + 1)], writes=[("sd", ts)])
                P.op("dve", (lambda ts=ts, bs=bs: lambda e: e.tensor_tensor(sa[ts], sa[ts], bank(bs + 2), ALU.mult))(),
                     reads=[pk(bs + 2), ("sa", ts)], writes=[("sa", ts)])
                P.op("dve", (lambda ts=ts, bs=bs: lambda e: e.tensor_tensor(sd[ts], sd[ts], bank(bs + 3), ALU.mult))(),
                     reads=[pk(bs + 3), ("sd", ts)], writes=[("sd", ts)])
                P.op("pool", (lambda ts=ts, c=c, tsl=tsl: lambda e: e.tensor_tensor(merged[:, c, tsl], sa[ts], sd[ts], ALU.add))(),
                     reads=[("sa", ts), ("sd", ts)], writes=[("mg", c, tb)])
            if c + 2 < 8:
                load_c(c + 2)
        wov = w_out_d[l].rearrange("(c p) e -> p c e", p=128)
        for hf in range(2):
            P.op("pool", dma(wo[:, :, hf * 512:(hf + 1) * 512], wov[:, :, hf * 512:(hf + 1) * 512]),
                 reads=([("wo", 0)] if hf == 1 else []),
                 writes=[("yT", 4), ("yT", 5), ("yT", 6), ("yT", 7), ("yTb", 4), ("yTb", 5), ("yTb", 6), ("yTb", 7), ("w1g", 1), ("w2g", 1), ("wo", hf)], dma=("wo", hf))

    def wout_phase(l, hook=None):
        it = 0
        order = [(tc, hf) for blk in range(2) for hf in range(2) for tc in range(blk * 8, blk * 8 + 8)]
        for (tc, hf) in order:
            if hook is not None and tc == 8 and hf == 0:
                hook()
            b = it % 8
            it += 1
            P.op("pe", mmk(bank(b), [(merged[:, c, tc * 128:(tc + 1) * 128], wo[:, c, hf * 512:(hf + 1) * 512]) for c in range(8)]),
                 reads=[("mg", c, tc // 4) for c in range(8)] + [("wo", hf)], writes=[pk(b)], hoist=True)
            P.op("dve", (lambda tc=tc, hf=hf, b=b: lambda e: e.tensor_tensor(xs[:, tc, hf * 512:(hf + 1) * 512], xs[:, tc, hf * 512:(hf + 1) * 512], bank(b), ALU.add))(),
                 reads=[pk(b), ("x", tc)], writes=[("x", tc)])

    def load_ffn_w(l, g):
        s = g % 2
        w1v = w_ff1_d[l].rearrange("(dc p) f -> p dc f", p=128)
        w2v = w_ff2_d[l].rearrange("(fc p) d -> p fc d", p=128)
        P.op("pool", dma(w1g[s], w1v[:, :, g * 512:(g + 1) * 512]),
             writes=[("w1g", s), ("yT", 4 * s), ("yT", 4 * s + 1), ("yTb", 4 * s), ("yTb", 4 * s + 1), ("wo", 0), ("wo", 1)], dma=("w1g", s))
        P.op("pool", dma(w2g[s], w2v[:, g * 4:(g + 1) * 4, :]),
             writes=[("w2g", s), ("yT", 4 * s + 2), ("yT", 4 * s + 3), ("yTb", 4 * s + 2), ("yTb", 4 * s + 3), ("wo", 0), ("wo", 1)], dma=("w2g", s))

    def ffn_phase(l, hook=None):
        steps = [(g, tb) for g in range(8) for tb in range(4)]
        f1 = [0]
        f2 = [0]

        def FF1(i):
            g, tb = steps[i]
            s = g % 2
            for fc in range(4):
                b = f1[0] % 4
                r = f1[0] % 2
                f1[0] += 1
                P.op("pe", mmk(bank(b), [(w1g[s][:, dc, fc * 128:(fc + 1) * 128], hT[:, dc, tb * 512:(tb + 1) * 512]) for dc in range(8)]),
                     reads=[("w1g", s)] + hT_keys(tb), writes=[pk(b)], hoist=True)
                P.op("act", (lambda r=r, b=b: lambda e: e.activation(rl[r], bank(b), AF.Relu))(),
                     reads=[pk(b)], writes=[("rl", r)])
                P.op("pool", (lambda r=r, s=s, fc=fc, tb=tb: lambda e: e.tensor_tensor(uT[s][:, fc, tb * 512:(tb + 1) * 512], rl[r], rl[r], ALU.mult))(),
                     reads=[("rl", r)], writes=[("uT", s, tb, fc)])

        def FF2(i):
            g, tb = steps[i]
            s = g % 2
            for j in range(4):
                tc = tb * 4 + j
                for hf in range(2):
                    b = 4 + f2[0] % 4
                    f2[0] += 1
                    P.op("pe", mmk(bank(b), [(uT[s][:, fc, tc * 128:(tc + 1) * 128], w2g[s][:, fc, hf * 512:(hf + 1) * 512]) for fc in range(4)]),
                         reads=[("w2g", s)] + [("uT", s, tb, fc) for fc in range(4)], writes=[pk(b)], hoist=True)
                    P.op("dve", (lambda tc=tc, hf=hf, b=b: lambda e: e.tensor_tensor(xs[:, tc, hf * 512:(hf + 1) * 512], xs[:, tc, hf * 512:(hf + 1) * 512], bank(b), ALU.add))(),
                         reads=[pk(b), ("x", tc)], writes=[("x", tc)])

        FF1(0)
        for i in range(len(steps)):
            if i + 1 < len(steps):
                FF1(i + 1)
            FF2(i)
            g, tb = steps[i]
            if tb == 3 and g + 2 < 8:
                load_ffn_w(l, g + 2)
            if hook is not None and g == 7:
                hook[0](tb)
                if tb >= 1:
                    hook[1](tb - 1)
        if hook is not None:
            hook[1](3)

    def dump_T(buf3):
        ov = out_d.rearrange("(c p h) d -> p c (h d)", c=8, p=128, h=2)
        for c in range(8):
            P.op("pool", dma(ov[:, c, :], buf3[:, c, :]), reads=[("hT", i) for i in range(16)] + [("yT", i) for i in range(8)] + [("yTb", i) for i in range(8)], dma="out")

    def dump_x():
        for g in range(4):
            P.op("sp", dma(out_d.rearrange("(t p) d -> p t d", p=128)[:, g * 4:(g + 1) * 4, :], xs[:, g * 4:(g + 1) * 4, :]),
                 reads=[("x", g * 4 + i) for i in range(4)], dma="out")

    def main():
        for li, l in enumerate(layers):
            if li == 0:
                load_unit_w(l, 0)
                load_unit_w(l, 1, after_x=True)
                for w in ("Et", "I", "Eb"):
                    load_na_bias(l, 0, w, after_x=True)
                norm_phase(n1g_d[l], False)
            if dbg == "hT":
                return dump_T(hT)
            P.op("sp", dma(dband, t5band_d), writes=[("dband", h) for h in range(4)], dma="dband")
            for h in range(4):
                P.op("dve", (lambda h=h: lambda e: e.tensor_scalar_sub(dband[:, h, :], dband[:, h, :], cst[:, h:h + 1]))(),
                     reads=[("dband", h), "cst"], writes=[("dband", h)])

            def next_bias(h, l=l):
                if h + 1 < 8:
                    return (l, h + 1)
                return None

            pend_y = None
            for u in range(8):
                if u <= 4:
                    projection(l, u)
                    if pend_y is not None:
                        y_transposes(pend_y)
                        pend_y = None
                if u == 4:
                    P.op("dve", lambda e: e.memset(qTxb[0:64, :, 256:512], 0.0), writes=["qTx0b", "bI", "bEt", "bEb"])
                    P.op("dve", lambda e: e.memset(qTxb[64:128, :, 0:256], 0.0), writes=["qTx0b", "bI", "bEt", "bEb"])
                if u < 4:
                    na_attention(l, u, next_bias)
                else:
                    nxt = proj_chunks(l, u + 1, dset=(u + 1 - 4) % 2) if u + 1 < 8 else ()
                    diff_attention(l, u - 4, nxt, prev_y=pend_y)
                pend_y = u
                if u + 2 < 8:
                    load_unit_w(l, u + 2)
            y_transposes(pend_y)
            if dbg == "yT":
                return dump_T(yT)
            phase_c(l, preload_only=True)
            P.barrier()
            phase_c(l)
            if dbg == "x1":
                wout_phase(l)
                return dump_x()
            n2_stats, n2_apply = norm_phase(n2g_d[l], False, hooked=True)
            wout_phase(l, lambda: (n2_stats(0), n2_stats(1)))
            load_ffn_w(l, 0)
            load_ffn_w(l, 1)
            n2_stats(2)
            n2_apply(0)
            n2_stats(3)
            n2_apply(1)
            n2_apply(2)
            n2_apply(3)
            if li + 1 < len(layers):
                ln_ = layers[li + 1]
                load_unit_w(ln_, 0)
                load_unit_w(ln_, 1)
            P.barrier()
            if li + 1 < len(layers):
                for w in ("Et", "I", "Eb"):
                    load_na_bias(layers[li + 1], 0, w)
            if li + 1 < len(layers):
                ffn_phase(l, norm_phase(n1g_d[layers[li + 1]], False, hooked=True))
                P.barrier()
            elif final:
                ffn_phase(l, norm_phase(fg_d, True, hooked=True))
            else:
                ffn_phase(l)
                dump_x()

    main()
    P.emit(final_waits=["out"])
    return nc


_CACHE = {}


def _get_nc(layers, final):
    key = (tuple(layers), final)
    if key not in _CACHE:
        _CACHE[key] = build(list(layers), final)
    return _CACHE[key]


FUSED = True


def kernel(x, t5_bias, final_norm_g, norm1_g, w_in, na_rpb, diff_lambda, diff_subln_g,
           w_na_o, w_diff_o, w_out, norm2_g, w_ff1, w_ff2):
    f = lambda a: np.ascontiguousarray(np.asarray(a, dtype=np.float32))
    x = f(x)
    t5band, t5c, nab = _host_tables(f(t5_bias), f(na_rpb))
    common = {
        "w_in": f(w_in), "w_na_o": f(w_na_o), "w_diff_o": f(w_diff_o), "w_out": f(w_out),
        "w_ff1": f(w_ff1), "w_ff2": f(w_ff2), "norm1_g": f(norm1_g), "norm2_g": f(norm2_g),
        "final_norm_g": f(final_norm_g), "diff_lambda": f(diff_lambda).reshape(512),
        "diff_subln_g": f(diff_subln_g).reshape(256), "t5c": t5c, "t5band": t5band, "nab": nab,
        "ident": np.eye(128, dtype=np.float32),
    }
    n = 8
    cur = [x[b] for b in range(n)]
    plan = [((0, 1), True)] if FUSED else [((0,), False), ((1,), True)]
    for layers, final in plan:
        nc = _get_nc(layers, final)
        in_maps = [dict(common, x=np.ascontiguousarray(cur[b])) for b in range(n)]
        res = run_bass_kernel_spmd(nc, in_maps, core_ids=list(range(n)))
        cur = [np.asarray(res.results[b]["out"], dtype=np.float32) for b in range(n)]
    return np.stack(cur, axis=0)
```
